# Optimizing a Trainium2 kernel written in Bass

```python
import math
import jax, jax.numpy as jnp
from jax import lax
import numpy as np

D_MODEL = 2048
BATCH = 4
SEQ = 2048
DEPTH = 2

HEAD_DIM = 128
A_HEADS = 4
A_WIDTH = A_HEADS * HEAD_DIM
A_BLOCK = 256
A_TOPK = 3
A_QCHUNK = 32
RPE_BUCKETS = 32
RPE_MAX_DIST = 128
B_HEADS = 4
B_DK = 128
B_DV = 256
B_KWIDTH = B_HEADS * B_DK
B_VWIDTH = B_HEADS * B_DV
B_LOWRANK = 16
B_GATE_NORM = 16.0
B_CHUNK = 64
C_GROUPS = 4
C_GROUP_DIM = 128
C_WIDTH = C_GROUPS * C_GROUP_DIM
C_CHUNK = 128
N_BRANCH = 3
D_FF = 4 * D_MODEL
EPS = 1e-6
NEG_INF = -1e30
IN_WIDTHS = (A_WIDTH, A_WIDTH, A_WIDTH, B_KWIDTH, B_KWIDTH, B_VWIDTH, B_VWIDTH, B_LOWRANK, C_WIDTH, C_WIDTH, D_MODEL, D_MODEL, D_MODEL)
IN_WIDTH = sum(IN_WIDTHS)

kernel_name = "hybrid_moba_gla_gmlp_block"


def rms_norm(x, g):
    xf = x.astype(jnp.float32)
    y = xf * lax.rsqrt(jnp.mean(xf * xf, axis=-1, keepdims=True) + EPS)
    return (y * g.astype(jnp.float32)).astype(x.dtype)


def layer_norm(x, g, b):
    xf = x.astype(jnp.float32)
    mu = jnp.mean(xf, axis=-1, keepdims=True)
    xc = xf - mu
    y = xc * lax.rsqrt(jnp.mean(xc * xc, axis=-1, keepdims=True) + EPS)
    return (y * g.astype(jnp.float32) + b.astype(jnp.float32)).astype(x.dtype)


def rpe_bucket(dist):
    n = jnp.maximum(dist, 0)
    max_exact = RPE_BUCKETS // 2
    nf = jnp.maximum(n, 1).astype(jnp.float32)
    large = max_exact + (jnp.log(nf / max_exact) / math.log(RPE_MAX_DIST / max_exact)
                         * (RPE_BUCKETS - max_exact)).astype(jnp.int32)
    large = jnp.minimum(large, RPE_BUCKETS - 1)
    return jnp.where(n < max_exact, n, large)


def moba_attention(q, k, v, rpe_table):
    bsz, seq, nh, hd = q.shape
    nb = -(-seq // A_BLOCK)
    sp = nb * A_BLOCK
    pad = ((0, 0), (0, sp - seq), (0, 0), (0, 0))
    q, k, v = (jnp.pad(t, pad).transpose(0, 2, 1, 3) for t in (q, k, v))
    kb = k.reshape(bsz, nh, nb, A_BLOCK, hd)
    vb = v.reshape(bsz, nh, nb, A_BLOCK, hd)
    kmean = jnp.mean(kb.astype(jnp.float32), axis=3)
    qblk = jnp.arange(sp, dtype=jnp.int32) // A_BLOCK
    past = jnp.arange(nb, dtype=jnp.int32)[None, :] < qblk[:, None]
    gate = jnp.einsum('bhtd,bhnd->bhtn', q.astype(jnp.float32), kmean)
    gate = jnp.where(past, gate, -jnp.inf)
    ksel = min(A_TOPK, nb)
    _, idx = lax.top_k(gate, ksel)
    idx = idx.astype(jnp.int32)
    valid = idx < qblk[None, None, :, None]
    nqc = sp // A_QCHUNK

    def to_chunks(t):
        return jnp.moveaxis(t.reshape(bsz, nh, nqc, A_QCHUNK, *t.shape[3:]), 2, 0)

    bt = rpe_table.T.astype(jnp.float32)
    b_ix = jnp.arange(bsz)[:, None, None, None]
    h_ix = jnp.arange(nh)[None, :, None, None]
    scale = hd ** -0.5
    offs = jnp.arange(A_BLOCK, dtype=jnp.int32)

    def chunk_attend(args):
        qc, ic, vc, c = args
        t = c * A_QCHUNK + jnp.arange(A_QCHUNK, dtype=jnp.int32)
        own = (c * A_QCHUNK) // A_BLOCK
        k_own = lax.dynamic_index_in_dim(kb, own, axis=2, keepdims=False)
        v_own = lax.dynamic_index_in_dim(vb, own, axis=2, keepdims=False)
        k_sel = kb[b_ix, h_ix, ic]
        v_sel = vb[b_ix, h_ix, ic]
        kpos_sel = ic[..., None] * A_BLOCK + offs
        s_sel = jnp.einsum('bhqd,bhqnkd->bhqnk', qc, k_sel).astype(jnp.float32) * scale
        s_sel = s_sel + bt[h_ix[..., None], rpe_bucket(t[:, None, None] - kpos_sel)]
        s_sel = jnp.where(vc[..., None], s_sel, NEG_INF).reshape(bsz, nh, A_QCHUNK, ksel * A_BLOCK)
        kpos_own = own * A_BLOCK + offs
        s_own = jnp.einsum('bhqd,bhkd->bhqk', qc, k_own).astype(jnp.float32) * scale
        s_own = s_own + bt[:, rpe_bucket(t[:, None] - kpos_own[None, :])][None]
        s_own = jnp.where(kpos_own[None, :] <= t[:, None], s_own, NEG_INF)
        p = jax.nn.softmax(jnp.concatenate([s_sel, s_own], axis=-1), axis=-1).astype(v.dtype)
        p_sel = p[..., :ksel * A_BLOCK].reshape(bsz, nh, A_QCHUNK, ksel, A_BLOCK)
        p_own = p[..., ksel * A_BLOCK:]
        return (jnp.einsum('bhqnk,bhqnkd->bhqd', p_sel, v_sel)
                + jnp.einsum('bhqk,bhkd->bhqd', p_own, v_own))

    out = lax.map(chunk_attend, (to_chunks(q), to_chunks(idx), to_chunks(valid),
                                 jnp.arange(nqc, dtype=jnp.int32)))
    out = jnp.moveaxis(out, 0, 2).reshape(bsz, nh, sp, hd).transpose(0, 2, 1, 3)[:, :seq]
    return out.reshape(bsz, seq, nh * hd)


def gla_chunked(q, k, v, log_a):
    bsz, seq, nh, dk = q.shape
    dv = v.shape[-1]
    nc = seq // B_CHUNK

    def to_chunks(t):
        return t.astype(jnp.float32).reshape(bsz, nc, B_CHUNK, nh, t.shape[-1]).transpose(1, 0, 3, 2, 4)

    qc, kc, vc, gc = to_chunks(q), to_chunks(k), to_chunks(v), to_chunks(log_a)
    bcum = jnp.cumsum(gc, axis=3)
    q_dec = qc * jnp.exp(bcum) * dk ** -0.5
    k_inv = kc * jnp.exp(-bcum)
    b_last = bcum[:, :, :, -1:, :]
    k_end = kc * jnp.exp(b_last - bcum)
    causal = jnp.tril(jnp.ones((B_CHUNK, B_CHUNK), dtype=bool))
    a_intra = jnp.where(causal, jnp.einsum('nbhtd,nbhsd->nbhts', q_dec, k_inv), 0.0)
    o_intra = jnp.einsum('nbhts,nbhsv->nbhtv', a_intra, vc)
    states = jnp.einsum('nbhsd,nbhsv->nbhdv', k_end, vc)
    decay = jnp.exp(b_last[:, :, :, 0, :])

    def step(s_prev, inp):
        d, s_c = inp
        return d[..., None] * s_prev + s_c, s_prev

    _, s_before = lax.scan(step, jnp.zeros((bsz, nh, dk, dv), jnp.float32), (decay, states))
    o = o_intra + jnp.einsum('nbhtd,nbhdv->nbhtv', q_dec, s_before)
    return o.transpose(1, 0, 3, 2, 4).reshape(bsz, seq, nh, dv).astype(v.dtype)


def spatial_gating(u, v, ln_g, ln_b, w_s, b_s):
    bsz, seq, _ = u.shape
    u = jax.nn.gelu(u)
    v = layer_norm(jax.nn.gelu(v), ln_g, ln_b)
    nch = seq // C_CHUNK
    vg = v.reshape(bsz, nch, C_CHUNK, C_GROUPS, C_GROUP_DIM)
    w = jnp.where(jnp.tril(jnp.ones((C_CHUNK, C_CHUNK), dtype=bool)), w_s, 0.0)
    mixed = jnp.einsum('gts,bnsgc->bntgc', w, vg) + b_s.T[:, :, None]
    return u * mixed.reshape(bsz, seq, C_WIDTH)


def hybrid_layer(x, rpe_table, n1, w_in, qn, kn, lr_w, lr_b, on_g, ln_g, ln_b, sg_w, sg_b,
                 wa, wb, wc, w_o, n2, w1, w2):
    bsz, seq, _ = x.shape
    xn = rms_norm(x, n1)
    proj = xn @ w_in
    splits = np.cumsum(IN_WIDTHS[:-1]).tolist()
    qa, ka, va, qb, kb, vb, rb, lrb, uc, vc, ga, gb, gc = jnp.split(proj, splits, axis=-1)
    qa = rms_norm(qa.reshape(bsz, seq, A_HEADS, HEAD_DIM), qn)
    ka = rms_norm(ka.reshape(bsz, seq, A_HEADS, HEAD_DIM), kn)
    va = va.reshape(bsz, seq, A_HEADS, HEAD_DIM)
    y_a = moba_attention(qa, ka, va, rpe_table)
    log_a = jax.nn.log_sigmoid((lrb @ lr_w + lr_b).astype(jnp.float32)) / B_GATE_NORM
    o_b = gla_chunked(qb.reshape(bsz, seq, B_HEADS, B_DK), kb.reshape(bsz, seq, B_HEADS, B_DK),
                      vb.reshape(bsz, seq, B_HEADS, B_DV), log_a.reshape(bsz, seq, B_HEADS, B_DK))
    y_b = rms_norm(o_b, on_g).reshape(bsz, seq, B_VWIDTH) * jax.nn.silu(rb)
    y_c = spatial_gating(uc, vc, ln_g, ln_b, sg_w, sg_b)
    merged = (jax.nn.sigmoid(ga) * (y_a @ wa) + jax.nn.sigmoid(gb) * (y_b @ wb)
              + jax.nn.sigmoid(gc) * (y_c @ wc))
    h = x + merged @ w_o
    hn = rms_norm(h, n2)
    return h + jnp.square(jax.nn.relu(hn @ w1)) @ w2


def setup_inputs(seed: int = 0) -> dict:
    key = jax.random.key(seed)
    ks = jax.random.split(key, 24)
    L = DEPTH

    def nrm(k, shape, scale):
        return jax.random.normal(k, shape, jnp.float32) * scale

    return {
        "x": nrm(ks[0], (BATCH, SEQ, D_MODEL), 1.0),
        "rpe_table": nrm(ks[1], (RPE_BUCKETS, A_HEADS), 0.5),
        "norm1_g": 1.0 + nrm(ks[2], (L, D_MODEL), 0.02),
        "w_in": nrm(ks[3], (L, D_MODEL, IN_WIDTH), D_MODEL ** -0.5),
        "q_norm_g": 1.0 + nrm(ks[4], (L, HEAD_DIM), 0.02),
        "k_norm_g": 1.0 + nrm(ks[5], (L, HEAD_DIM), 0.02),
        "gla_lr_w": nrm(ks[6], (L, B_LOWRANK, B_KWIDTH), B_LOWRANK ** -0.5),
        "gla_lr_b": nrm(ks[7], (L, B_KWIDTH), 0.1),
        "gla_out_g": 1.0 + nrm(ks[8], (L, B_DV), 0.02),
        "sg_ln_g": 1.0 + nrm(ks[9], (L, C_WIDTH), 0.02),
        "sg_ln_b": nrm(ks[10], (L, C_WIDTH), 0.02),
        "sg_w": nrm(ks[11], (L, C_GROUPS, C_CHUNK, C_CHUNK), C_CHUNK ** -0.5),
        "sg_b": 1.0 + nrm(ks[12], (L, C_GROUPS, C_CHUNK), 0.02),
        "w_br_a": nrm(ks[13], (L, A_WIDTH, D_MODEL), A_WIDTH ** -0.5),
        "w_br_b": nrm(ks[14], (L, B_VWIDTH, D_MODEL), B_VWIDTH ** -0.5),
        "w_br_c": nrm(ks[15], (L, C_WIDTH, D_MODEL), C_WIDTH ** -0.5),
        "w_o": nrm(ks[16], (L, D_MODEL, D_MODEL), D_MODEL ** -0.5),
        "norm2_g": 1.0 + nrm(ks[17], (L, D_MODEL), 0.02),
        "w_ff1": nrm(ks[18], (L, D_MODEL, D_FF), D_MODEL ** -0.5),
        "w_ff2": nrm(ks[19], (L, D_FF, D_MODEL), D_FF ** -0.5),
    }


def reference(x, rpe_table, norm1_g, w_in, q_norm_g, k_norm_g, gla_lr_w, gla_lr_b, gla_out_g,
              sg_ln_g, sg_ln_b, sg_w, sg_b, w_br_a, w_br_b, w_br_c, w_o, norm2_g, w_ff1, w_ff2):
    h = x
    for l in range(DEPTH):
        h = hybrid_layer(h, rpe_table, norm1_g[l], w_in[l], q_norm_g[l], k_norm_g[l],
                         gla_lr_w[l], gla_lr_b[l], gla_out_g[l], sg_ln_g[l], sg_ln_b[l],
                         sg_w[l], sg_b[l], w_br_a[l], w_br_b[l], w_br_c[l], w_o[l],
                         norm2_g[l], w_ff1[l], w_ff2[l])
    return h
```

```python
import math
import numpy as np
from contextlib import ExitStack
import concourse.bass as bass
import concourse.mybir as mybir
from concourse.bass_utils import run_bass_kernel_spmd

F32 = mybir.dt.float32
BF16 = mybir.dt.bfloat16
AF = mybir.ActivationFunctionType
ALU = mybir.AluOpType
AX = mybir.AxisListType

L = 2
D = 2048
S = 2048
NT = 1024
T = 1024
NG = NT // T
INW = 11792
DFF = 8192
EPS = 1e-6
C_QA, C_KA, C_VA, C_QB, C_KB, C_VB, C_RB, C_LR, C_UC, C_VC, C_G = (
    0, 512, 1024, 1536, 2048, 2560, 3584, 4608, 4624, 5136, 5648)
K_ID, K_J, K_ONE, K_TRI, K_UN, K_LS, K_EN, K_E32, K_NEG, K_PM, K_END = 0, 128, 256, 384, 512, 640, 768, 1792, 2176, 2560, 2816
NPRM = 40
GROUPS = [[0, 1], [2, 3], [4, 5], [6, 7]]


class Buf:
    __slots__ = ("w", "r")

    def __init__(self):
        self.w = None
        self.r = []


class DSem:
    __slots__ = ("sem", "count", "no_fence")

    def __init__(self, sem):
        self.sem = sem
        self.count = 0
        self.no_fence = False


class Op:
    __slots__ = ("eng", "idx", "fn", "deps", "sig", "semval", "dsem", "waits", "is_dma", "inc")

    def __init__(self, eng, fn):
        self.eng = eng
        self.fn = fn
        self.deps = []
        self.sig = False
        self.semval = None
        self.dsem = None
        self.waits = []
        self.is_dma = False
        self.idx = -1
        self.inc = 16


class V:
    __slots__ = ("ap", "bufs")

    def __init__(self, ap, bufs):
        self.ap = ap
        self.bufs = bufs


def DV(ap):
    return V(ap, [])


class Tl:
    def __init__(self, t):
        self.t = t
        self.buf = Buf()

    def __getitem__(self, idx):
        return V(self.t[idx], [self.buf])

    def v(self, ap):
        return V(ap, [self.buf])


ENGS = ("pe", "act", "dve", "pool", "sp")


class Prog:
    def __init__(self, nc, stack):
        self.nc = nc
        self.stack = stack
        self.streams = {e: [] for e in ENGS}
        self.sems = {e: stack.enter_context(nc.semaphore("prog_" + e)) for e in ENGS if e != "sp"}
        self.uid = 0
        self.fence = []
        self.last_compute = {}
        self.last_dma = {}
        self.dsems = []
        self.arenas = {}

    def sb(self, shape, dtype):
        self.uid += 1
        return Tl(self.stack.enter_context(self.nc.sbuf_tensor(f"t{self.uid}", list(shape), dtype)))

    def ps(self, shape, dtype=F32):
        self.uid += 1
        return Tl(self.stack.enter_context(self.nc.psum_tensor(f"p{self.uid}", list(shape), dtype)))

    def dsem(self):
        self.uid += 1
        d = DSem(self.stack.enter_context(self.nc.semaphore(f"ds{self.uid}")))
        self.dsems.append(d)
        return d

    def make_arena(self, name, nbytes):
        self.arenas[name] = [self.sb([128, nbytes // 2], BF16), 0, nbytes // 2]

    def alloc(self, arena, n, dtype):
        a = self.arenas[arena]
        units = n if dtype == BF16 else 2 * n
        units = (units + 15) // 16 * 16
        off = a[1]
        assert off + units <= a[2], f"arena {arena} overflow: {off}+{units}>{a[2]}"
        a[1] = off + units
        ap = a[0].t[:, off:off + (n if dtype == BF16 else 2 * n)]
        if dtype != BF16:
            ap = ap.bitcast(dtype)
        return Tl(ap)

    def release(self, *names):
        for nme in names:
            self.arenas[nme][1] = 0
        self.barrier()

    def barrier(self):
        f = [op for op in self.last_compute.values()]
        f += [op for op in self.last_dma.values() if not op.dsem.no_fence]
        self.fence = f

    def _record(self, eng, fn, reads, writes, is_dma=False, dsem=None, inc=16):
        op = Op(eng, fn)
        op.inc = inc
        op.is_dma = is_dma
        op.dsem = dsem
        deps = list(self.fence)
        for v in reads:
            for b in v.bufs:
                if b.w is not None:
                    deps.append(b.w)
        for v in writes:
            for b in v.bufs:
                if b.w is not None:
                    deps.append(b.w)
                deps.extend(b.r)
        seen = set()
        for d in deps:
            if id(d) in seen:
                continue
            seen.add(id(d))
            if (not is_dma) and eng == "pe" and d.eng == "pe" and not d.is_dma:
                continue
            op.deps.append(d)
        for v in reads:
            for b in v.bufs:
                b.r.append(op)
        for v in writes:
            for b in v.bufs:
                b.w = op
                b.r = []
        if is_dma:
            dsem.count += inc
            op.semval = dsem.count
            op.sig = True
            self.last_dma[id(dsem)] = op
        else:
            self.last_compute[eng] = op
        op.idx = len(self.streams[eng])
        self.streams[eng].append(op)
        return op

    def mm(self, out, lhsT, rhs, start=True, stop=True):
        return self._record("pe", lambda e: e.matmul(out.ap, lhsT.ap, rhs.ap, start=start, stop=stop), [lhsT, rhs], [out])

    def transpose(self, out, in_, ident):
        return self._record("pe", lambda e: e.transpose(out.ap, in_.ap, ident.ap), [in_, ident], [out])

    def act(self, out, in_, func, bias=None, scale=None, accum_out=None):
        kw = {}
        reads = [in_]
        writes = [out]
        if bias is not None:
            if isinstance(bias, V):
                kw["bias"] = bias.ap
                reads.append(bias)
            else:
                kw["bias"] = bias
        if scale is not None:
            if isinstance(scale, V):
                kw["scale"] = scale.ap
                reads.append(scale)
            else:
                kw["scale"] = scale
        if accum_out is not None:
            kw["accum_out"] = accum_out.ap
            writes.append(accum_out)
        return self._record("act", lambda e: e.activation(out.ap, in_.ap, func, **kw), reads, writes)

    def tt(self, out, in0, in1, op, eng="dve"):
        return self._record(eng, lambda e: e.tensor_tensor(out.ap, in0.ap, in1.ap, op), [in0, in1], [out])

    def ts(self, out, in0, s1, s2, op0, op1=None, eng="dve"):
        reads = [in0]
        a1 = s1.ap if isinstance(s1, V) else s1
        a2 = s2.ap if isinstance(s2, V) else s2
        if isinstance(s1, V):
            reads.append(s1)
        if isinstance(s2, V):
            reads.append(s2)
        if op1 is None:
            return self._record(eng, lambda e: e.tensor_single_scalar(out.ap, in0.ap, a1, op0), reads, [out])
        return self._record(eng, lambda e: e.tensor_scalar(out.ap, in0.ap, a1, a2, op0, op1), reads, [out])

    def stt(self, out, in0, scalar, in1, op0, op1, eng="dve"):
        reads = [in0, in1]
        a = scalar.ap if isinstance(scalar, V) else scalar
        if isinstance(scalar, V):
            reads.append(scalar)
        return self._record(eng, lambda e: e.scalar_tensor_tensor(out.ap, in0.ap, a, in1.ap, op0, op1), reads, [out])

    def copy(self, out, in_, eng="dve"):
        if eng == "act":
            return self._record(eng, lambda e: e.copy(out.ap, in_.ap), [in_], [out])
        return self._record(eng, lambda e: e.tensor_copy(out.ap, in_.ap), [in_], [out])

    def memset(self, out, val, eng="dve"):
        return self._record(eng, lambda e: e.memset(out.ap, val), [], [out])

    def reduce(self, out, in_, op, axis=AX.X, eng="dve"):
        return self._record(eng, lambda e: e.tensor_reduce(out.ap, in_.ap, axis, op), [in_], [out])

    def recip(self, out, in_):
        return self._record("dve", lambda e: e.reciprocal(out.ap, in_.ap), [in_], [out])

    def generic(self, eng, fn, reads, writes):
        return self._record(eng, fn, reads, writes)

    def dma(self, out, in_, dsem, q="sp", **kw):
        return self._record(q, lambda e: e.dma_start(out=out.ap, in_=in_.ap, **kw), [in_], [out], is_dma=True, dsem=dsem)

    def finalize(self):
        for eng in ENGS:
            known = {}
            for op in self.streams[eng]:
                for d in op.deps:
                    if d.is_dma:
                        key = ("d", id(d.dsem))
                        if known.get(key, 0) >= d.semval:
                            continue
                        known[key] = d.semval
                        op.waits.append(d)
                    else:
                        key = ("e", d.eng)
                        if known.get(key, -1) >= d.idx:
                            continue
                        known[key] = d.idx
                        d.sig = True
                        op.waits.append(d)
        for eng in ENGS:
            c = 0
            for op in self.streams[eng]:
                if op.is_dma:
                    continue
                if op.sig:
                    c += 1
                    op.semval = c
        self.stats = {e: len(self.streams[e]) for e in ENGS}

    def emit(self):
        self.finalize()
        nc = self.nc
        prog = self
        with nc.Block() as block:
            def body(engname):
                def run(e):
                    for op in prog.streams[engname]:
                        for d in op.waits:
                            if d.is_dma:
                                e.wait_ge(d.dsem.sem, d.semval)
                            else:
                                e.wait_ge(prog.sems[d.eng], d.semval)
                        ins = op.fn(e)
                        if op.is_dma:
                            ins.then_inc(op.dsem.sem, op.inc)
                        elif op.sig:
                            ins.then_inc(prog.sems[op.eng], 1)
                    if engname == "sp":
                        for ds in prog.dsems:
                            if ds.count:
                                e.wait_ge(ds.sem, ds.count)
                return run
            block.tensor(body("pe"))
            block.scalar(body("act"))
            block.vector(body("dve"))
            block.gpsimd(body("pool"))
            block.sync(body("sp"))


def build_program(debug=(), nlayers=L, stop_after=None, skip_inputs=()):
    nc = bass.Bass("TRN2", target_bir_lowering=False)

    def din(name, shape, dt=F32):
        kind = "Internal" if name in skip_inputs else "ExternalInput"
        return nc.dram_tensor(name, list(shape), dt, kind=kind).ap()

    def scr(name, shape, dt):
        kind = "ExternalOutput" if name in debug else "Internal"
        return nc.dram_tensor(name, list(shape), dt, kind=kind).ap()

    xT = din("xT", [D, NT])
    w_in = din("w_in", [L, D, INW])
    w_a = din("w_br_a", [L, 512, D])
    w_b = din("w_br_b", [L, 1024, D])
    w_c = din("w_br_c", [L, 512, D])
    w_o = din("w_o", [L, D, D])
    w_1 = din("w_ff1", [L, D, DFF])
    w_2 = din("w_ff2", [L, DFF, D])
    rpe = din("rpe", [32, 4])
    cst_d = din("cst", [128, K_END])
    prm_d = din("prm", [128, L * NPRM])
    bc_d = din("bc", [L, 128, 3072])
    lrw_d = din("lrw", [L, 128, 512])
    sgw_d = din("sgw", [L, 128, 512])
    flags_d = din("flags", [128, 2])
    outT = nc.dram_tensor("outT", [D, NT], F32, kind="ExternalOutput").ap()

    hbuf = scr("hbuf", [D, NT], F32)
    qa_d = scr("qa_d", [512, NT], F32)
    ka_d = scr("ka_d", [512, NT], F32)
    va_d = scr("va_d", [NT, 512], BF16)
    qb_d = scr("qb_d", [512, NT], F32)
    kbF_d = scr("kbF_d", [512, NT], F32)
    kbT_d = scr("kbT_d", [NT, 512], F32)
    vb_d = scr("vb_d", [NT, 1024], BF16)
    rb_d = scr("rb_d", [1024, NT], BF16)
    lr_d = scr("lr_d", [16, NT], F32)
    uc_d = scr("uc_d", [512, NT], BF16)
    vc_d = scr("vc_d", [NT, 512], F32)
    gt_d = scr("gt_d", [6144, NT], BF16)
    ya_d = scr("ya_d", [512, NT], BF16)
    yb_d = scr("yb_d", [1024, NT], BF16)
    yc_d = scr("yc_d", [512, NT], BF16)
    kh_in = [scr(f"kh_in{l}", [512, NT], BF16) for l in range(L)]
    kh_out = [scr(f"kh_out{l}", [1024, NT], BF16) for l in range(L)]
    v_in = [scr(f"v_in{l}", [NT, 512], BF16) for l in range(L)]
    v_out = [scr(f"v_out{l}", [2 * NT, 512], BF16) for l in range(L)]
    st_in = [scr(f"st_in{l}", [512, 256], F32) for l in range(L)]
    st_out = [scr(f"st_out{l}", [1024, 256], F32) for l in range(L)]
    b_khin = [Buf() for _ in range(L)]
    b_khout = [Buf() for _ in range(L)]
    b_vin = [Buf() for _ in range(L)]
    b_vout = [Buf() for _ in range(L)]
    b_stin = [Buf() for _ in range(L)]
    b_stout = [Buf() for _ in range(L)]
    vv_t = nc.dram_tensor("vv_d", [4, 384], F32, kind="Internal")
    vv_d = vv_t.ap()

    with ExitStack() as st:
        P = Prog(nc, st)
        cf = P.sb([128, 512], F32)
        cb = P.sb([128, K_E32], BF16)
        prm = P.sb([128, L * NPRM], F32)
        Fb = P.sb([128, 8, 128], BF16)
        flags = P.sb([128, 2], F32)
        pm = P.sb([128, 256], F32)
        tri4 = P.sb([128, 512], F32)
        kmean_all = P.sb([128, 32], F32)
        pan = [P.sb([128, 8192], BF16) for _ in range(2)]
        pan_ds = [P.dsem() for _ in range(2)]
        stf = [P.sb([128, 512], F32) for _ in range(2)]
        stf_ds = [P.dsem() for _ in range(2)]
        stb = [P.sb([128, 512], BF16) for _ in range(2)]
        stb_ds = [P.dsem() for _ in range(2)]
        P.make_arena("X", 64 * 1024)
        P.make_arena("Y", 56 * 1024)
        banks = [P.ps([128, 512], F32) for _ in range(8)]
        NMISC = 48
        misc_ds = [P.dsem() for _ in range(NMISC)]
        st_ctr = {"f": 0, "b": 0, "p": 0, "bank": 0, "m": 0}

        ident_f = cf[:, 0:128]
        tri_f = cf[:, 128:256]
        uneg_f = cf[:, 256:384]
        lstr_f = cf[:, 384:512]
        ident_b = cb[:, K_ID:K_ID + 128]
        J_b = cb[:, K_J:K_J + 128]
        ones_b = cb[:, K_ONE:K_ONE + 128]

        reserved = []

        def nbank():
            while True:
                st_ctr["bank"] = (st_ctr["bank"] + 1) % 8
                b_ = banks[st_ctr["bank"]]
                if not any(b_ is r_ for r_ in reserved):
                    return b_

        def mds():
            st_ctr["m"] = (st_ctr["m"] + 1) % NMISC
            return misc_ds[st_ctr["m"]]

        def stage(kind):
            if kind == "f":
                i = st_ctr["f"] = (st_ctr["f"] + 1) % 2
                return stf[i], stf_ds[i]
            i = st_ctr["b"] = (st_ctr["b"] + 1) % 2
            return stb[i], stb_ds[i]

        def load_panel(parts):
            i = st_ctr["p"] = (st_ctr["p"] + 1) % 2
            slot, ds = pan[i], pan_ds[i]
            ktot = sum(p[1] for p in parts)
            cw = parts[0][2].shape[2]
            view = slot.t[:, 0:ktot * cw].rearrange("p (k c) -> p k c", k=ktot)
            for (k0, kc, src) in parts:
                P.dma(slot.v(view[:, k0:k0 + kc, :]), DV(src), ds, q="pool")
            return slot, view

        def wview(w2d):
            return w2d.rearrange("(k p) c -> p k c", p=128)

        hres0 = [P.alloc("X", T, F32) for _ in range(16)]
        for k_ in range(16):
            P.dma(hres0[k_][:], DV(xT[k_ * 128:(k_ + 1) * 128, 0:T]), mds())
        c_all = P.alloc("Y", K_END, F32)
        P.dma(c_all[:], DV(cst_d), mds())
        P.dma(prm[:], DV(prm_d), mds())
        P.dma(flags[:], DV(flags_d), mds())
        P.copy(cb[:], c_all[:, 0:K_E32])
        P.copy(cf[:, 0:128], c_all[:, K_ID:K_ID + 128])
        P.copy(cf[:, 128:256], c_all[:, K_TRI:K_TRI + 128])
        P.copy(cf[:, 256:512], c_all[:, K_UN:K_UN + 256])
        P.copy(pm[:], c_all[:, K_PM:K_PM + 256])
        for h_ in range(4):
            P.copy(tri4[:, h_ * 128:(h_ + 1) * 128], c_all[:, K_TRI:K_TRI + 128])
        tab = P.alloc("Y", 4, F32)
        P.dma(tab.v(tab.t[0:32, :]), DV(rpe), mds())
        bk = nbank()
        P.mm(bk.v(bk.t[0:4, 0:384]), tab.v(tab.t[0:32, 0:4]), c_all.v(c_all.t[0:32, K_E32:K_E32 + 384]), start=True, stop=False)
        P.mm(bk.v(bk.t[0:4, 0:384]), c_all.v(c_all.t[0:1, K_ONE:K_ONE + 4]), c_all.v(c_all.t[0:1, K_NEG:K_NEG + 384]), start=False, stop=True)
        vvs = P.alloc("Y", 384, F32)
        P.copy(vvs.v(vvs.t[0:4, :]), bk.v(bk.t[0:4, 0:384]))
        vds = mds()
        P.dma(DV(vv_d), vvs.v(vvs.t[0:4, :]), vds)
        P.barrier()
        Ff = P.alloc("Y", 8 * 128, F32)
        fds = mds()
        for h in range(4):
            for dl in range(2):
                hap = bass.AP(vv_t, 384 * h + 1 + 128 * dl, [[1, 128], [1, 128]])
                P.dma(Ff[:, (h * 2 + dl) * 128:(h * 2 + dl + 1) * 128], DV(hap), fds)
        P.copy(Fb.v(Fb.t[:].rearrange("p a b -> p (a b)")), Ff[:])
        P.release("Y")

        def phase1(l, g, hres=None):
            tok0 = g * T
            hsrc = xT if l == 0 else hbuf
            po = l * NPRM
            if hres is None:
                hres = [P.alloc("X", T, F32) for _ in range(16)]
                for k in range(16):
                    P.dma(hres[k][:], DV(hsrc[k * 128:(k + 1) * 128, tok0:tok0 + T]), mds())
            xn = [P.alloc("Y", T, BF16) for _ in range(16)]
            R = P.alloc("Y", T, F32)
            sq = [P.alloc("Y", T, BF16) for _ in range(2)]
            ssb = [nbank(), nbank()]
            for k in range(16):
                P.act(sq[k % 2][:], hres[k][:], AF.Square)
                for sub in range(T // 512):
                    P.mm(ssb[sub][:], ones_b, sq[k % 2][:, sub * 512:(sub + 1) * 512], start=(k == 0), stop=(k == 15))
            for sub in range(T // 512):
                P.act(R[:, sub * 512:(sub + 1) * 512], ssb[sub][:], AF.Sqrt, bias=EPS, scale=1.0 / D)
            P.recip(R[:], R[:])
            for k in range(16):
                P.stt(xn[k][:], hres[k][:], prm[:, po + k:po + k + 1], R[:], ALU.mult, ALU.mult)
            P.release("X")

            wv = wview(w_in[l])
            ev_ctr = [0]

            def evac(kind, bank_v, rows, cols):
                if kind in ("copy32", "gelu32"):
                    stg, ds = stage("f")
                else:
                    stg, ds = stage("b")
                o = stg.v(stg.t[0:rows, 0:cols])
                if kind in ("copy32", "copy16"):
                    ev_ctr[0] += 1
                    P.copy(o, bank_v, eng="act" if ev_ctr[0] % 2 else "dve")
                elif kind == "silu":
                    P.act(o, bank_v, AF.Silu)
                elif kind == "sigmoid":
                    P.act(o, bank_v, AF.Sigmoid)
                else:
                    P.act(o, bank_v, AF.Gelu_apprx_tanh)
                return o, ds

            def fm_seg(c0, width, kind, dst, r0=0):
                for pc in range(0, width, 512):
                    cw = min(512, width - pc)
                    slot, view = load_panel([(0, 16, wv[:, :, c0 + pc:c0 + pc + cw])])
                    for ct in range(0, cw, 128):
                        m = min(128, cw - ct)
                        for sub in range(T // 512):
                            bk = nbank()
                            for k in range(16):
                                P.mm(bk.v(bk.t[0:m, :]), slot.v(view[:, k, ct:ct + m]), xn[k][:, sub * 512:(sub + 1) * 512],
                                     start=(k == 0), stop=(k == 15))
                            o, ds = evac(kind, bk.v(bk.t[0:m, :]), m, 512)
                            P.dma(DV(dst[r0 + pc + ct:r0 + pc + ct + m, tok0 + sub * 512:tok0 + (sub + 1) * 512]), o, ds)

            def tm_seg(c0, width, kind, dst):
                for pc in range(0, width, 512):
                    slot, view = load_panel([(0, 16, wv[:, :, c0 + pc:c0 + pc + 512])])
                    for tt_ in range(T // 128):
                        bk = nbank()
                        for k in range(16):
                            P.mm(bk[:], xn[k][:, tt_ * 128:(tt_ + 1) * 128], slot.v(view[:, k, :]), start=(k == 0), stop=(k == 15))
                        o, ds = evac(kind, bk[:], 128, 512)
                        P.dma(DV(dst[tok0 + tt_ * 128:tok0 + (tt_ + 1) * 128, pc:pc + 512]), o, ds)

            fm_seg(C_QA, 512, "copy32", qa_d)
            fm_seg(C_KA, 512, "copy32", ka_d)
            tm_seg(C_VA, 512, "copy16", v_in[l])
            fm_seg(C_QB, 512, "copy32", qb_d)
            fm_seg(C_KB, 512, "copy32", kbF_d)
            tm_seg(C_KB, 512, "copy32", kbT_d)
            tm_seg(C_VB, 1024, "copy16", vb_d)
            fm_seg(C_RB, 1024, "silu", rb_d)
            fm_seg(C_LR, 16, "copy32", lr_d)
            fm_seg(C_UC, 512, "gelu16", uc_d)
            tm_seg(C_VC, 512, "gelu32", vc_d)
            fm_seg(C_G, 6144, "sigmoid", gt_d)
            P.release("X", "Y")

        def cc_gather(src, dst, bsrc, bdst):
            ds = P.dsem()
            ds.no_fence = True
            P._record("pool", lambda e: e.collective_compute("AllGather", ALU.bypass, replica_groups=GROUPS,
                                                             ins=[src.opt()], outs=[dst.opt()]),
                      [V(src, [bsrc])], [V(dst, [bdst])], is_dma=True, dsem=ds, inc=1)

        def knorm(l):
            po = l * NPRM
            H4 = range(4)
            kf = [P.alloc("X", NT, F32) for _ in H4]
            sqb = [P.alloc("X", NT, BF16) for _ in H4]
            rr = [P.alloc("X", NT, F32) for _ in H4]
            kds = [mds() for _ in H4]
            for h in H4:
                P.dma(kf[h][:], DV(ka_d[h * 128:(h + 1) * 128, :]), kds[h])
            for h in H4:
                P.act(sqb[h][:], kf[h][:], AF.Square)
            for h in H4:
                for sub in range(NT // 512):
                    bk = nbank()
                    P.mm(bk[:], ones_b, sqb[h][:, sub * 512:(sub + 1) * 512])
                    P.act(rr[h][:, sub * 512:(sub + 1) * 512], bk[:], AF.Sqrt, bias=EPS, scale=1.0 / 128.0)
            for h in H4:
                P.recip(rr[h][:], rr[h][:])
            for h in H4:
                P.stt(kf[h][:], kf[h][:], prm[:, po + 33:po + 34], rr[h][:], ALU.mult, ALU.mult)
            for h in H4:
                for sub in range(NT // 512):
                    stg, ds = stage("b")
                    P.copy(stg[:], kf[h][:, sub * 512:(sub + 1) * 512], eng="act")
                    P.dma(DV(kh_in[l][h * 128:(h + 1) * 128, sub * 512:(sub + 1) * 512]), stg[:], ds)
            for h in H4:
                P.reduce(kmean_all[:, h * 8 + 4:h * 8 + 8], kf[h].v(kf[h].t[:].rearrange("p (n c) -> p n c", n=4)), ALU.add)
            P.release("X", "Y")

        def moba(l):
            po = l * NPRM
            V_all = P.alloc("X", 16 * 512, BF16)
            V3 = V_all.t[:].rearrange("p (n c) -> p n c", n=16)
            P.dma(V_all.v(V3[:, 0:8, :]), V(v_out[l][0:NT, :].rearrange("(n p) c -> p n c", p=128), [b_vout[l]]), mds())
            P.dma(V_all.v(V3[:, 8:16, :]), DV(v_in[l].rearrange("(n p) c -> p n c", p=128)), mds())
            qf4 = P.alloc("Y", 4 * NT, F32)
            qf3 = qf4.t[:].rearrange("p (h s) -> p h s", h=4)
            qb4 = P.alloc("X", 4 * NT, BF16)
            qb3 = qb4.t[:].rearrange("p (h s) -> p h s", h=4)
            kb4 = P.alloc("X", 4 * S, BF16)
            kb3 = kb4.t[:].rearrange("p (h s) -> p h s", h=4)
            sqb = [P.alloc("X", NT, BF16) for _ in range(4)]
            rr = [P.alloc("Y", NT, F32) for _ in range(4)]
            selT = P.alloc("X", 32 * 128, BF16)
            P.memset(selT[:], 0.0)
            PT = [P.alloc("Y", 512, BF16) for _ in range(3)]
            rec = P.alloc("Y", 512, F32)
            G_all = P.alloc("Y", 256, F32)
            vld = P.alloc("Y", 256, F32)
            sel_all = P.alloc("Y", 256, F32)
            sel_i = [Tl(sel_all.t[:, i * 8:(i + 1) * 8]) for i in range(32)]
            M8_all = P.alloc("Y", 256, F32)
            M8_i = [Tl(M8_all.t[:, i * 8:(i + 1) * 8]) for i in range(32)]
            P.dma(qf4.v(qf3), DV(qa_d.rearrange("(h p) s -> p h s", p=128)), mds())
            P.dma(kb4.v(kb3[:, :, 0:NT]), V(kh_out[l][0:512, :].rearrange("(h p) s -> p h s", p=128), [b_khout[l]]), mds())
            P.dma(kb4.v(kb3[:, :, NT:S]), DV(kh_in[l].rearrange("(h p) s -> p h s", p=128)), mds())
            H4 = range(4)
            qhs = [qf4.v(qf3[:, h, :]) for h in H4]
            for h in H4:
                P.act(sqb[h][:], qhs[h], AF.Square)
            for h in H4:
                for sub in range(NT // 512):
                    bk = nbank()
                    P.mm(bk[:], ones_b, sqb[h][:, sub * 512:(sub + 1) * 512])
                    P.act(rr[h][:, sub * 512:(sub + 1) * 512], bk[:], AF.Sqrt, bias=128.0 * EPS, scale=1.0)
            for h in H4:
                P.recip(rr[h][:], rr[h][:])
            for h in H4:
                P.stt(qhs[h], qhs[h], prm[:, po + 32:po + 33], rr[h][:], ALU.mult, ALU.mult)
            for h in H4:
                P.copy(qb4.v(qb3[:, h, :]), qhs[h], eng="act")
            for h in H4:
                P.reduce(kmean_all[:, h * 8:h * 8 + 4], kb4.v(kb3[:, h, 0:NT].rearrange("p (n c) -> p n c", n=4)), ALU.add)
            bG = nbank()
            for h in range(4):
                for lq in range(8):
                    i = h * 8 + lq
                    P.mm(bG[:, i * 8:(i + 1) * 8], qf4.v(qf3[:, h, lq * 128:(lq + 1) * 128]), kmean_all[:, h * 8:h * 8 + 8])
            P.tt(G_all[:], bG[:, 0:256], pm[:], ALU.add)
            Gv = G_all.t[:].rearrange("p (i n) -> p i n", n=8)
            P.ts(G_all.v(Gv[:, :, 0:4]), G_all.v(Gv[:, :, 0:4]), flags[:, 1:2], None, ALU.add)
            for i in range(32):
                P.generic("dve", lambda e, o=M8_i[i], i_=i: e.max(o.t, G_all.t[:, i_ * 8:(i_ + 1) * 8]), [G_all[:]], [M8_i[i][:]])
            for i in range(32):
                P.ts(sel_i[i][:], G_all[:, i * 8:(i + 1) * 8], M8_i[i][:, 2:3], None, ALU.is_ge)
            P.ts(vld[:], G_all[:], -1e29, None, ALU.is_gt)
            sel_full = V(sel_all.t[:], [t_.buf for t_ in sel_i])
            P.tt(sel_full, sel_full, vld[:], ALU.mult)
            P.ts(sel_full, sel_full, -1.0, 1e30, ALU.add, ALU.mult)
            for i4 in range(8):
                bT = nbank()
                for j in range(4):
                    i = i4 * 4 + j
                    P.transpose(bT.v(bT.t[0:8, j * 128:(j + 1) * 128]), V(sel_all.t[:, i * 8:(i + 1) * 8], [t_.buf for t_ in sel_i]), ident_f)
                P.copy(selT.v(selT.t[0:8, i4 * 512:(i4 + 1) * 512]), bT.v(bT.t[0:8, 0:512]), eng=("act" if i4 % 2 else "dve"))
            steps = []
            for h in range(4):
                for g in (2, 3):
                    nk = 4 * g + 4
                    for kt in range(nk):
                        steps.append((h, g, kt, nk))
            acc = {}

            def issue_scores(h, g, kt, nk):
                if kt == 0:
                    acc[(h, g)] = (nbank(), nbank())
                OT, DEN = acc[(h, g)]
                live = [b_ for pair in acc.values() for b_ in pair]
                qend = (4 * g + 4 - 8) * 128
                qlo = max(kt, 4 * g)
                c0 = (qlo - 4 * g) * 128
                STb = nbank()
                while any(STb is b_ for b_ in live):
                    STb = nbank()
                mms = [(STb[:, c0:512], kb4.v(kb3[:, h, kt * 128:(kt + 1) * 128]), qb4.v(qb3[:, h, (qlo - 8) * 128:qend]))]
                if kt >= 4 * g:
                    cc = (kt - 4 * g) * 128
                    mms.append((STb[:, cc:cc + 128], J_b, Fb.v(Fb.t[:, h * 2 + 0, :])))
                if 4 * g <= kt + 1 <= 4 * g + 3:
                    cc = (kt + 1 - 4 * g) * 128
                    mms.append((STb[:, cc:cc + 128], J_b, Fb.v(Fb.t[:, h * 2 + 1, :])))
                n = kt // 2
                qs = max(qlo, 2 * n + 2)
                if qs <= 4 * g + 3:
                    cc = (qs - 4 * g) * 128
                    mms.append((STb[:, cc:512], cb[:, K_EN + n * 128:K_EN + (n + 1) * 128],
                                selT[:, (h * 8 + qs - 8) * 128:h * 1024 + qend]))
                for i, (o_, l_, r_) in enumerate(mms):
                    P.mm(o_, l_, r_, start=(i == 0), stop=(i == len(mms) - 1))
                return STb, c0

            def issue_pv(h, g, kt, nk, STb, c0, idx):
                OT, DEN = acc[(h, g)]
                pt = PT[idx % 3]
                P.act(pt[:, c0:512], STb[:, c0:512], AF.Exp)
                P.mm(OT[:, c0:512], V_all.v(V3[:, kt, h * 128:(h + 1) * 128]), pt[:, c0:512], start=(kt == 0), stop=(kt == nk - 1))
                P.mm(DEN[:, c0:512], ones_b, pt[:, c0:512], start=(kt == 0), stop=(kt == nk - 1))
                if kt == nk - 1:
                    P.recip(rec[:], DEN[:])
                    stg, ds = stage("b")
                    P.tt(stg[:], OT[:], rec[:], ALU.mult)
                    P.dma(DV(ya_d[h * 128:(h + 1) * 128, (g - 2) * 512:(g - 1) * 512]), stg[:], ds)
                    del acc[(h, g)]

            pend = None
            for idx, stp in enumerate(steps):
                cur = issue_scores(*stp)
                if pend is not None:
                    issue_pv(*pend)
                pend = (*stp, cur[0], cur[1], idx)
            issue_pv(*pend)
            P.release("X", "Y")

        NCH = NT // 128

        def gla_gen(l, state_only, rel=True):
            po = l * NPRM
            la = P.alloc("Y", NCH * 512, F32)
            la3 = la.t[:].rearrange("p (n c) -> p n c", n=NCH)
            lrT = P.alloc("X", NT, F32)
            lrw = P.alloc("X", 512, F32)
            etmp = [P.alloc("X", 512, F32) for _ in range(2)]
            P.memset(lrT[:], 1.0)
            P.dma(lrT.v(lrT.t[0:16, :]), DV(lr_d), mds())
            P.dma(lrw[:], DV(lrw_d[l]), mds())
            la_c = [Tl(la3[:, c, :]) for c in range(NCH)]
            for tt_ in range(NCH):
                bk = nbank()
                P.mm(bk[:], lrT[:, tt_ * 128:(tt_ + 1) * 128], lrw[:])
                P.act(etmp[tt_ % 2][:], bk[:], AF.Exp, scale=-1.0)
                P.act(la_c[tt_][:], etmp[tt_ % 2][:], AF.Ln, bias=1.0, scale=1.0)
            ktc = [P.alloc("X", 512, F32) for _ in range(2)]
            vtc = [P.alloc("X", 1024, BF16) for _ in range(2)]
            E3 = [P.alloc("Y", 512, F32) for _ in range(2)]
            ke = [P.alloc("Y", 512, BF16) for _ in range(2)]
            Sf = P.alloc("Y", 1024, F32)
            Sf_h = [Tl(Sf.t[:, h * 256:(h + 1) * 256]) for h in range(4)]
            Sf_full = V(Sf.t[:], [t_.buf for t_ in Sf_h])
            lds = [[mds() for _ in range(2)] for _ in range(5)]
            if state_only:
                dec = [P.alloc("Y", 4, F32) for _ in range(2)]
                P.memset(Sf_full, 0.0)
            else:
                qfc = [P.alloc("X", 512, F32) for _ in range(2)]
                kfc = [P.alloc("X", 512, F32) for _ in range(2)]
                rTc = [P.alloc("X", 1024, BF16) for _ in range(2)]
                t1 = [P.alloc("X", 1024, F32) for _ in range(2)]
                ybc = [P.alloc("X", 1024, BF16) for _ in range(2)]
                yb_ds = [mds(), mds()]
                E1 = [P.alloc("Y", 512, F32) for _ in range(2)]
                E2 = [P.alloc("Y", 512, F32) for _ in range(2)]
                qd = [P.alloc("Y", 512, BF16) for _ in range(2)]
                ki = [P.alloc("Y", 512, BF16) for _ in range(2)]
                am = [P.alloc("Y", 512, BF16) for _ in range(2)]
                Sb = P.alloc("Y", 1024, BF16)
                sqo = [P.alloc("Y", 1024, BF16) for _ in range(2)]
                rr = [P.alloc("Y", 512, F32) for _ in range(2)]
                sds = mds()
                P.dma(V(Sf.t[:].rearrange("p (h c) -> p h c", h=4), Sf_full.bufs), V(st_out[l][0:512, :].rearrange("(h p) c -> p h c", p=128), [b_stout[l]]), sds)
                P.ts(Sf_full, Sf_full, flags[:, 0:1], None, ALU.mult)
                P.copy(Sb[:], Sf_full, eng="act")
            if not state_only:
                o_sb = [P.alloc("X", 1024, F32) for _ in range(2)]

            def stage_f(c):
                pr = c % 2
                cs = slice(c * 128, (c + 1) * 128)
                P.dma(ktc[pr][:], DV(kbT_d[cs, :]), lds[0][pr])
                P.dma(vtc[pr][:], DV(vb_d[cs, :]), lds[1][pr])
                if not state_only:
                    P.dma(qfc[pr].v(qfc[pr].t[:].rearrange("p (h s) -> p h s", h=4)), DV(qb_d.rearrange("(h p) s -> p h s", p=128)[:, :, cs]), lds[2][pr])
                    P.dma(kfc[pr].v(kfc[pr].t[:].rearrange("p (h s) -> p h s", h=4)), DV(kbF_d.rearrange("(h p) s -> p h s", p=128)[:, :, cs]), lds[3][pr])
                lac = la_c[c]
                bB = banks[0]
                for h in range(4):
                    P.mm(bB[:, h * 128:(h + 1) * 128], lstr_f, lac[:, h * 128:(h + 1) * 128])
                bA = banks[1]
                if state_only:
                    for h in range(4):
                        P.mm(bA[:, h:h + 1], lac[:, h * 128:(h + 1) * 128], cf[:, 383:384])
                else:
                    for h in range(4):
                        P.mm(bA[:, h * 128:(h + 1) * 128], lac[:, h * 128:(h + 1) * 128], uneg_f)
                P.act(E3[pr][:], bB[:], AF.Exp)
                P.tt(ke[pr][:], ktc[pr][:], E3[pr][:], ALU.mult)
                if state_only:
                    P.act(dec[pr][:], bA[:, 0:4], AF.Exp)
                    decay = [dec[pr][:, h:h + 1] for h in range(4)]
                else:
                    P.act(E1[pr][:], bA[:], AF.Exp)
                    P.act(E2[pr][:], bA[:], AF.Exp, scale=-1.0)
                    decay = [E1[pr][:, h * 128 + 127:h * 128 + 128] for h in range(4)]
                    P.stt(qd[pr][:], qfc[pr][:], 128.0 ** -0.5, E1[pr][:], ALU.mult, ALU.mult)
                    P.tt(ki[pr][:], kfc[pr][:], E2[pr][:], ALU.mult)

            def stage_m(c):
                pr = c % 2
                if state_only:
                    decay = [dec[pr][:, h:h + 1] for h in range(4)]
                else:
                    decay = [E1[pr][:, h * 128 + 127:h * 128 + 128] for h in range(4)]
                if not state_only:
                    cs = slice(c * 128, (c + 1) * 128)
                    P.dma(rTc[pr].v(rTc[pr].t[:].rearrange("p (j s) -> p j s", j=8)), DV(rb_d.rearrange("(j p) s -> p j s", p=128)[:, :, cs]), lds[4][pr])
                    bC = banks[2]
                    for h in range(4):
                        hs = slice(h * 128, (h + 1) * 128)
                        P.mm(bC[:, hs], ki[pr][:, hs], qd[pr][:, hs])
                    P.tt(am[pr][:], bC[:], tri4[:], ALU.mult)
                    bO = [banks[3], banks[4]]
                    for h in range(4):
                        hs = slice(h * 128, (h + 1) * 128)
                        for dv in range(2):
                            o_ = bO[h // 2][:, (h % 2) * 256 + dv * 128:(h % 2) * 256 + (dv + 1) * 128]
                            P.mm(o_, vtc[pr][:, h * 256 + dv * 128:h * 256 + (dv + 1) * 128], am[pr][:, hs], start=True, stop=False)
                            P.mm(o_, Sb[:, h * 256 + dv * 128:h * 256 + (dv + 1) * 128], qd[pr][:, hs], start=False, stop=True)
                bS = [banks[5], banks[6]]
                for h in range(4):
                    P.mm(bS[h // 2][:, (h % 2) * 256:(h % 2 + 1) * 256], ke[pr][:, h * 128:(h + 1) * 128], vtc[pr][:, h * 256:(h + 1) * 256])
                for h in range(4):
                    P.stt(Sf_h[h][:], Sf_h[h][:], decay[h], bS[h // 2][:, (h % 2) * 256:(h % 2 + 1) * 256], ALU.mult, ALU.add)
                if state_only:
                    return
                P.copy(Sb[:], Sf_full, eng="act")
                for half in range(2):
                    P.act(sqo[pr][:, half * 512:(half + 1) * 512], bO[half][:], AF.Square)
                    P.copy(o_sb[pr][:, half * 512:(half + 1) * 512], bO[half][:], eng="act")

            def stage_b(c):
                pr = c % 2
                cs = slice(c * 128, (c + 1) * 128)
                bR = banks[7]
                for h in range(4):
                    P.mm(bR[:, h * 128:(h + 1) * 128], ones_b, sqo[pr][:, h * 256:h * 256 + 128], start=True, stop=False)
                    P.mm(bR[:, h * 128:(h + 1) * 128], ones_b, sqo[pr][:, h * 256 + 128:h * 256 + 256], start=False, stop=True)
                P.act(rr[pr][:], bR[:], AF.Sqrt, bias=EPS, scale=1.0 / 256.0)
                P.recip(rr[pr][:], rr[pr][:])
                t1v = t1[pr].t[:].rearrange("p (a b c) -> p a b c", a=4, b=2)
                ov = o_sb[pr].t[:].rearrange("p (a b c) -> p a b c", a=4, b=2)
                rrv = rr[pr].t[:].rearrange("p (a c) -> p a c", a=4)
                for dv in range(2):
                    P.stt(t1[pr].v(t1v[:, :, dv, :]), o_sb[pr].v(ov[:, :, dv, :]),
                          prm[:, po + 34 + dv:po + 35 + dv], rr[pr].v(rrv), ALU.mult, ALU.mult)
                P.tt(ybc[pr][:], t1[pr][:], rTc[pr][:], ALU.mult)
                P.dma(DV(yb_d.rearrange("(j p) s -> p j s", p=128)[:, :, cs]), ybc[pr].v(ybc[pr].t[:].rearrange("p (j s) -> p j s", j=8)), yb_ds[pr])

            yield
            if state_only:
                stage_f(0)
                for c in range(NCH):
                    if c + 1 < NCH:
                        stage_f(c + 1)
                    stage_m(c)
                    yield
            else:
                stage_f(0)
                for c in range(NCH):
                    if c + 1 < NCH:
                        stage_f(c + 1)
                    stage_m(c)
                    if c >= 1:
                        stage_b(c - 1)
                stage_b(NCH - 1)
            if state_only:
                P.dma(DV(st_in[l].rearrange("(h p) c -> p h c", p=128)), V(Sf.t[:].rearrange("p (h c) -> p h c", h=4), Sf_full.bufs), mds())
            if rel:
                P.release("X", "Y")

        def gla(l, state_only):
            for _ in gla_gen(l, state_only):
                pass

        def gmlp_gen(l, rel=True):
            ucT = P.alloc("X", 4 * NT, BF16)
            uc3 = ucT.t[:].rearrange("p (g s) -> p g s", g=4)
            P.dma(ucT.v(uc3), DV(uc_d.rearrange("(g p) s -> p g s", p=128)), mds())
            bcl = P.alloc("X", 3072, F32)
            P.dma(bcl[:], DV(bc_d[l]), mds())
            ws = P.alloc("X", 512, F32)
            P.dma(ws[:], DV(sgw_d[l]), mds())
            wsb = P.alloc("X", 512, BF16)
            P.tt(wsb[:], ws[:], tri4[:], ALU.mult)
            vt = [P.alloc("Y", 512, F32) for _ in range(4)]
            vds = [mds() for _ in range(4)]
            junk = [P.alloc("Y", 512, BF16) for _ in range(4)]
            vn = [P.alloc("Y", 512, F32) for _ in range(4)]
            vnb = [P.alloc("Y", 512, BF16) for _ in range(4)]
            tmp = [P.alloc("Y", 512, F32) for _ in range(2)]
            s1 = [P.alloc("Y", 1, F32) for _ in range(4)]
            s2 = [P.alloc("Y", 1, F32) for _ in range(4)]
            mu = [P.alloc("Y", 1, F32) for _ in range(4)]
            m2 = [P.alloc("Y", 1, F32) for _ in range(4)]
            var = [P.alloc("Y", 1, F32) for _ in range(4)]
            R4 = range(4)
            yield
            for c4 in range(NCH // 4):
                gb = [nbank() for _ in range(4)]
                for cc in R4:
                    c = c4 * 4 + cc
                    P.dma(vt[cc][:], DV(vc_d[c * 128:(c + 1) * 128, :]), vds[cc])
                for cc in R4:
                    P.reduce(s1[cc][:], vt[cc][:], ALU.add)
                yield
                for cc in R4:
                    P.act(junk[cc][:], vt[cc][:], AF.Square, accum_out=s2[cc][:])
                for cc in R4:
                    P.ts(mu[cc][:], s1[cc][:], 1.0 / 512.0, None, ALU.mult)
                yield
                for cc in R4:
                    P.tt(m2[cc][:], mu[cc][:], mu[cc][:], ALU.mult)
                for cc in R4:
                    P.stt(var[cc][:], s2[cc][:], 1.0 / 512.0, m2[cc][:], ALU.mult, ALU.subtract)
                yield
                for cc in R4:
                    P.act(var[cc][:], var[cc][:], AF.Sqrt, bias=EPS, scale=1.0)
                for cc in R4:
                    P.recip(var[cc][:], var[cc][:])
                yield
                for cc in R4:
                    P.ts(vn[cc][:], vt[cc][:], mu[cc][:, 0:1], var[cc][:, 0:1], ALU.subtract, ALU.mult)
                for cc in R4:
                    P.tt(vn[cc][:], vn[cc][:], bcl[:, 0:512], ALU.mult)
                yield
                for cc in R4:
                    P.tt(vnb[cc][:], vn[cc][:], bcl[:, 512:1024], ALU.add)
                for cc in R4:
                    for gi in range(4):
                        P.mm(gb[gi][:, cc * 128:(cc + 1) * 128], vnb[cc][:, gi * 128:(gi + 1) * 128], wsb[:, gi * 128:(gi + 1) * 128])
                yield
                for gi in range(4):
                    P.tt(tmp[gi % 2][:], gb[gi][:], bcl[:, 1024 + gi * 512:1024 + (gi + 1) * 512], ALU.add)
                    stg, ds = stage("b")
                    P.tt(stg[:], tmp[gi % 2][:], ucT.v(uc3[:, gi, c4 * 512:(c4 + 1) * 512]), ALU.mult)
                    P.dma(DV(yc_d[gi * 128:(gi + 1) * 128, c4 * 512:(c4 + 1) * 512]), stg[:], ds)
                yield
            if rel:
                P.release("X", "Y")

        def gmlp(l):
            for _ in gmlp_gen(l):
                pass

        def gla_gmlp(l):
            reserved.extend([banks[0], banks[1], banks[5], banks[6]])
            alive = [gla_gen(l, True, rel=False), gmlp_gen(l, rel=False)]
            while alive:
                for g_ in list(alive):
                    try:
                        next(g_)
                    except StopIteration:
                        alive.remove(g_)
            del reserved[:]
            P.release("X", "Y")

        def phase3(l, g, last, pre=None):
            tok0 = g * T
            po = l * NPRM
            hsrc = xT if l == 0 else hbuf
            hdst = outT if last else hbuf
            NS = T // 512
            y = [P.alloc("X", T, BF16) for _ in range(16)]
            for k in range(16):
                yd = mds()
                if k < 4:
                    src = ya_d[k * 128:(k + 1) * 128, tok0:tok0 + T]
                elif k < 12:
                    src = yb_d[(k - 4) * 128:(k - 3) * 128, tok0:tok0 + T]
                else:
                    src = yc_d[(k - 12) * 128:(k - 11) * 128, tok0:tok0 + T]
                P.dma(y[k][:], DV(src), yd)
            mg = [P.alloc("Y", T, BF16) for _ in range(16)]
            G6 = P.alloc("Y", 6 * 512, BF16)
            gsl = [Tl(G6.t[:, i * 512:(i + 1) * 512]) for i in range(6)]
            gds = [mds() for _ in range(6)]
            gi_ = 0

            def br_panel(cbk):
                return load_panel([(0, 4, wview(w_a[l])[:, :, cbk * 512:(cbk + 1) * 512]),
                                   (4, 8, wview(w_b[l])[:, :, cbk * 512:(cbk + 1) * 512]),
                                   (12, 4, wview(w_c[l])[:, :, cbk * 512:(cbk + 1) * 512])])
            nxt = pre if pre is not None else br_panel(0)
            TB = P.alloc("Y", 6 * 512, F32)
            tmpsets = [tuple(Tl(TB.t[:, (j * 3 + i) * 512:(j * 3 + i + 1) * 512]) for i in range(3)) for j in range(2)]
            it = 0
            for cbk in range(4):
                slot, view = nxt
                if cbk + 1 < 4:
                    nxt = br_panel(cbk + 1)
                for ct in range(4):
                    dt_ = cbk * 4 + ct
                    for sub in range(NS):
                        ss = slice(sub * 512, (sub + 1) * 512)
                        tmps = tmpsets[it % 2]
                        it += 1
                        for br, (k0, k1) in enumerate(((0, 4), (4, 12), (12, 16))):
                            gi_ = (gi_ + 1) % 6
                            gt = gsl[gi_]
                            P.dma(gt[:], DV(gt_d[br * 2048 + dt_ * 128: br * 2048 + (dt_ + 1) * 128, tok0 + sub * 512:tok0 + (sub + 1) * 512]), gds[gi_])
                            bk = nbank()
                            for k in range(k0, k1):
                                P.mm(bk[:], slot.v(view[:, k, ct * 128:(ct + 1) * 128]), y[k][:, ss], start=(k == k0), stop=(k == k1 - 1))
                            P.tt(tmps[br][:], bk[:], gt[:], ALU.mult)
                        P.tt(tmps[0][:], tmps[0][:], tmps[1][:], ALU.add)
                        P.tt(mg[dt_][:, ss], tmps[0][:], tmps[2][:], ALU.add, eng="pool")
            wo_first = load_panel([(0, 16, wview(w_o[l])[:, :, 0:512])])
            P.release("X")
            hT = [P.alloc("X", T, F32) for _ in range(16)]
            hstg = [Tl(TB.t[:, i * 512:(i + 1) * 512]) for i in range(3)]
            hds = [mds() for _ in range(3)]
            sq = [Tl(G6.t[:, i * 1024:(i + 1) * 1024]) for i in range(2)]
            ssb = [nbank() for _ in range(NS)]
            reserved.extend(ssb)
            hi_ = 0
            nxt = wo_first
            for cbk in range(4):
                slot, view = nxt
                if cbk + 1 < 4:
                    nxt = load_panel([(0, 16, wview(w_o[l])[:, :, (cbk + 1) * 512:(cbk + 2) * 512])])
                for ct in range(4):
                    dt_ = cbk * 4 + ct
                    for sub in range(NS):
                        ss = slice(sub * 512, (sub + 1) * 512)
                        hi_ = (hi_ + 1) % 3
                        P.dma(hstg[hi_][:], DV(hsrc[dt_ * 128:(dt_ + 1) * 128, tok0 + sub * 512:tok0 + (sub + 1) * 512]), hds[hi_])
                        bk = nbank()
                        for k in range(16):
                            P.mm(bk[:], slot.v(view[:, k, ct * 128:(ct + 1) * 128]), mg[k][:, ss], start=(k == 0), stop=(k == 15))
                        P.tt(hT[dt_][:, ss], bk[:], hstg[hi_][:], ALU.add)
                    if dt_ >= 1:
                        for sub in range(NS):
                            P.mm(ssb[sub][:], ones_b, sq[(dt_ - 1) % 2][:, sub * 512:(sub + 1) * 512], start=(dt_ == 1), stop=False)
                    P.act(sq[dt_ % 2][:], hT[dt_][:], AF.Square)
            for sub in range(NS):
                P.mm(ssb[sub][:], ones_b, sq[15 % 2][:, sub * 512:(sub + 1) * 512], start=False, stop=True)
            ffn_first = load_panel([(0, 16, wview(w_1[l])[:, :, 0:512])])
            P.release("Y")
            del reserved[:]
            hn = [P.alloc("Y", T, BF16) for _ in range(16)]
            aT = [P.alloc("Y", T, BF16) for _ in range(8)]
            R2 = P.alloc("Y", T, F32)
            for sub in range(NS):
                P.act(R2[:, sub * 512:(sub + 1) * 512], ssb[sub][:], AF.Sqrt, bias=EPS, scale=1.0 / D)
            P.recip(R2[:], R2[:])
            for k in range(16):
                P.stt(hn[k][:], hT[k][:], prm[:, po + 16 + k:po + 17 + k], R2[:], ALU.mult, ALU.mult)
            rtmp = [stf[0], stf[1]]
            for fb in range(8):
                for half in range(2):
                    c0 = fb * 1024 + half * 512
                    if fb == 0 and half == 0:
                        slot, view = ffn_first
                    else:
                        slot, view = load_panel([(0, 16, wview(w_1[l])[:, :, c0:c0 + 512])])
                    for ct in range(4):
                        fc = half * 4 + ct
                        for sub in range(NS):
                            ss = slice(sub * 512, (sub + 1) * 512)
                            bk = nbank()
                            for k in range(16):
                                P.mm(bk[:], slot.v(view[:, k, ct * 128:(ct + 1) * 128]), hn[k][:, ss], start=(k == 0), stop=(k == 15))
                            rt = rtmp[(fc * NS + sub) % 2]
                            P.act(rt[:], bk[:], AF.Relu)
                            P.tt(aT[fc][:, ss], rt[:], rt[:], ALU.mult)
                for j in range(2):
                    slot, view = load_panel([(0, 8, wview(w_2[l])[:, fb * 8:(fb + 1) * 8, j * 1024:(j + 1) * 1024])])
                    for ct in range(8):
                        dt_ = j * 8 + ct
                        for sub in range(NS):
                            ss = slice(sub * 512, (sub + 1) * 512)
                            bk = nbank()
                            for k in range(8):
                                P.mm(bk[:], slot.v(view[:, k, ct * 128:(ct + 1) * 128]), aT[k][:, ss], start=(k == 0), stop=(k == 7))
                            P.tt(hT[dt_][:, ss], bk[:], hT[dt_][:, ss], ALU.add)
            ods = [mds() for _ in range(4)]
            for dt_ in range(16):
                P.dma(DV(hdst[dt_ * 128:(dt_ + 1) * 128, tok0:tok0 + T]), hT[dt_][:], ods[dt_ % 4])
            if last:
                P.release("X", "Y")
                return None
            P.release("Y")
            return hT

        def first_branch_panel(l):
            return load_panel([(0, 4, wview(w_a[l])[:, :, 0:512]),
                               (4, 8, wview(w_b[l])[:, :, 0:512]),
                               (12, 4, wview(w_c[l])[:, :, 0:512])])

        hres = hres0
        for l in range(nlayers):
            for g in range(NG):
                phase1(l, g, hres)
            if stop_after == ("p1", l):
                break
            cc_gather(v_in[l], v_out[l], b_vin[l], b_vout[l])
            knorm(l)
            cc_gather(kh_in[l], kh_out[l], b_khin[l], b_khout[l])
            gla_gmlp(l)
            cc_gather(st_in[l], st_out[l], b_stin[l], b_stout[l])
            moba(l)
            pre = None
            gla(l, False)
            if stop_after == ("p2", l):
                break
            for g in range(NG):
                hres = phase3(l, g, last=(l == nlayers - 1), pre=pre)
        P.emit()
        build_program.stats = P.stats
    return nc


def _rpe_bucket(d):
    n = np.maximum(d, 0)
    max_exact = 16
    nf = np.maximum(n, 1).astype(np.float32)
    large = max_exact + (np.log(nf / np.float32(max_exact)) / np.float32(math.log(128 / max_exact)) * np.float32(32 - max_exact)).astype(np.int32)
    large = np.minimum(large, 31)
    return np.where(n < max_exact, n, large)


def _consts():
    c = np.zeros((128, K_END), np.float32)
    i = np.arange(128)
    c[:, K_ID:K_ID + 128] = np.eye(128, dtype=np.float32)
    c[:, K_J:K_J + 128] = np.eye(128, dtype=np.float32)[::-1]
    c[:, K_ONE:K_ONE + 128] = 1.0
    tri = (i[:, None] <= i[None, :]).astype(np.float32)
    c[:, K_TRI:K_TRI + 128] = tri
    c[:, K_UN:K_UN + 128] = tri * (-1.0 / 16.0)
    c[:, K_LS:K_LS + 128] = (i[:, None] > i[None, :]).astype(np.float32) * (-1.0 / 16.0)
    for n in range(8):
        c[n, K_EN + n * 128:K_EN + (n + 1) * 128] = 1.0
    m = np.arange(384)
    d = m - 128
    bkt = _rpe_bucket(d)
    for mm_ in range(128, 384):
        c[bkt[mm_], K_E32 + mm_] += 1.0
        c[31, K_E32 + mm_] -= 1.0
    c[0, K_NEG:K_NEG + 128] = -1e30
    for h in range(4):
        for lq in range(8):
            qblk = (8 + lq) // 2
            for n in range(qblk, 8):
                c[:, K_PM + (h * 8 + lq) * 8 + n] = -1e30
    return c


_NC_CACHE = {}


def _prep_inputs(inp):
    f = lambda a: np.ascontiguousarray(np.asarray(a, dtype=np.float32))
    prm = np.zeros((128, L * NPRM), np.float32)
    bc = np.zeros((L, 128, 3072), np.float32)
    lrw = np.zeros((L, 128, 512), np.float32)
    sgw = np.zeros((L, 128, 512), np.float32)
    for l in range(L):
        o = l * NPRM
        prm[:, o:o + 16] = f(inp["norm1_g"])[l].reshape(16, 128).T
        prm[:, o + 16:o + 32] = f(inp["norm2_g"])[l].reshape(16, 128).T
        prm[:, o + 32] = f(inp["q_norm_g"])[l]
        prm[:, o + 33] = f(inp["k_norm_g"])[l]
        prm[:, o + 34:o + 36] = f(inp["gla_out_g"])[l].reshape(2, 128).T
        bc[l, :, 0:512] = f(inp["sg_ln_g"])[l][None, :]
        bc[l, :, 512:1024] = f(inp["sg_ln_b"])[l][None, :]
        for gi in range(4):
            bc[l, :, 1024 + gi * 512:1024 + (gi + 1) * 512] = np.tile(f(inp["sg_b"])[l, gi], 4)[None, :]
        lrw[l, 0:16] = f(inp["gla_lr_w"])[l]
        lrw[l, 16] = f(inp["gla_lr_b"])[l]
        for gi in range(4):
            sgw[l, :, gi * 128:(gi + 1) * 128] = f(inp["sg_w"])[l, gi].T
    shared = {
        "w_in": f(inp["w_in"]), "w_br_a": f(inp["w_br_a"]), "w_br_b": f(inp["w_br_b"]), "w_br_c": f(inp["w_br_c"]),
        "w_o": f(inp["w_o"]), "w_ff1": f(inp["w_ff1"]), "w_ff2": f(inp["w_ff2"]),
        "rpe": f(inp["rpe_table"]), "cst": _consts(), "prm": prm, "bc": bc, "lrw": lrw, "sgw": sgw,
    }
    x = f(inp["x"])
    in_maps = []
    for c in range(8):
        b, half = c // 2, c % 2
        m = dict(shared)
        m["xT"] = np.ascontiguousarray(x[b, half * NT:(half + 1) * NT].T)
        fl = np.zeros((128, 2), np.float32)
        fl[:, 0] = 1.0 if half == 1 else 0.0
        fl[:, 1] = 0.0 if half == 1 else -1e30
        m["flags"] = fl
        in_maps.append(m)
    return in_maps


def kernel(**inputs):
    in_maps = _prep_inputs(inputs)
    if "nc" not in _NC_CACHE:
        _NC_CACHE["nc"] = build_program()
    nc = _NC_CACHE["nc"]
    res = run_bass_kernel_spmd(nc, in_maps, core_ids=list(range(8)))
    out = np.empty((4, S, D), np.float32)
    for c in range(8):
        b, half = c // 2, c % 2
        out[b, half * NT:(half + 1) * NT, :] = np.asarray(res.results[c]["outT"]).T
    return out
```

```python
import math
import numpy as np
from contextlib import ExitStack
import concourse.bass as bass
import concourse.mybir as mybir
from concourse.bass_utils import run_bass_kernel_spmd

F32 = mybir.dt.float32
BF16 = mybir.dt.bfloat16
AF = mybir.ActivationFunctionType
ALU = mybir.AluOpType
AX = mybir.AxisListType

L = 2
D = 2048
S = 2048
NT = 1024
T = 1024
NG = NT // T
INW = 11792
DFF = 8192
EPS = 1e-6
C_QA, C_KA, C_VA, C_QB, C_KB, C_VB, C_RB, C_LR, C_UC, C_VC, C_G = (
    0, 512, 1024, 1536, 2048, 2560, 3584, 4608, 4624, 5136, 5648)
K_ID, K_J, K_ONE, K_TRI, K_UN, K_LS, K_EN, K_E32, K_NEG, K_PM, K_END = 0, 128, 256, 384, 512, 640, 768, 1792, 2176, 2560, 2816
NPRM = 40
GROUPS = [[0, 1], [2, 3], [4, 5], [6, 7]]


class Buf:
    __slots__ = ("w", "r")

    def __init__(self):
        self.w = None
        self.r = []


class DSem:
    __slots__ = ("sem", "count", "no_fence")

    def __init__(self, sem):
        self.sem = sem
        self.count = 0
        self.no_fence = False


class Op:
    __slots__ = ("eng", "idx", "fn", "deps", "sig", "semval", "dsem", "waits", "is_dma", "inc")

    def __init__(self, eng, fn):
        self.eng = eng
        self.fn = fn
        self.deps = []
        self.sig = False
        self.semval = None
        self.dsem = None
        self.waits = []
        self.is_dma = False
        self.idx = -1
        self.inc = 16


class V:
    __slots__ = ("ap", "bufs")

    def __init__(self, ap, bufs):
        self.ap = ap
        self.bufs = bufs


def DV(ap):
    return V(ap, [])


class Tl:
    def __init__(self, t):
        self.t = t
        self.buf = Buf()

    def __getitem__(self, idx):
        return V(self.t[idx], [self.buf])

    def v(self, ap):
        return V(ap, [self.buf])


ENGS = ("pe", "act", "dve", "pool", "sp")


class Prog:
    def __init__(self, nc, stack):
        self.nc = nc
        self.stack = stack
        self.streams = {e: [] for e in ENGS}
        self.sems = {e: stack.enter_context(nc.semaphore("prog_" + e)) for e in ENGS if e != "sp"}
        self.uid = 0
        self.fence = []
        self.last_compute = {}
        self.last_dma = {}
        self.dsems = []
        self.arenas = {}

    def sb(self, shape, dtype):
        self.uid += 1
        return Tl(self.stack.enter_context(self.nc.sbuf_tensor(f"t{self.uid}", list(shape), dtype)))

    def ps(self, shape, dtype=F32):
        self.uid += 1
        return Tl(self.stack.enter_context(self.nc.psum_tensor(f"p{self.uid}", list(shape), dtype)))

    def dsem(self):
        self.uid += 1
        d = DSem(self.stack.enter_context(self.nc.semaphore(f"ds{self.uid}")))
        self.dsems.append(d)
        return d

    def make_arena(self, name, nbytes):
        self.arenas[name] = [self.sb([128, nbytes // 2], BF16), 0, nbytes // 2]

    def alloc(self, arena, n, dtype):
        a = self.arenas[arena]
        units = n if dtype == BF16 else 2 * n
        units = (units + 15) // 16 * 16
        off = a[1]
        assert off + units <= a[2], f"arena {arena} overflow: {off}+{units}>{a[2]}"
        a[1] = off + units
        ap = a[0].t[:, off:off + (n if dtype == BF16 else 2 * n)]
        if dtype != BF16:
            ap = ap.bitcast(dtype)
        return Tl(ap)

    def release(self, *names):
        for nme in names:
            self.arenas[nme][1] = 0
        self.barrier()

    def barrier(self):
        f = [op for op in self.last_compute.values()]
        f += [op for op in self.last_dma.values() if not op.dsem.no_fence]
        self.fence = f

    def _record(self, eng, fn, reads, writes, is_dma=False, dsem=None, inc=16):
        op = Op(eng, fn)
        op.inc = inc
        op.is_dma = is_dma
        op.dsem = dsem
        deps = list(self.fence)
        for v in reads:
            for b in v.bufs:
                if b.w is not None:
                    deps.append(b.w)
        for v in writes:
            for b in v.bufs:
                if b.w is not None:
                    deps.append(b.w)
                deps.extend(b.r)
        seen = set()
        for d in deps:
            if id(d) in seen:
                continue
            seen.add(id(d))
            if (not is_dma) and eng == "pe" and d.eng == "pe" and not d.is_dma:
                continue
            op.deps.append(d)
        for v in reads:
            for b in v.bufs:
                b.r.append(op)
        for v in writes:
            for b in v.bufs:
                b.w = op
                b.r = []
        if is_dma:
            dsem.count += inc
            op.semval = dsem.count
            op.sig = True
            self.last_dma[id(dsem)] = op
        else:
            self.last_compute[eng] = op
        op.idx = len(self.streams[eng])
        self.streams[eng].append(op)
        return op

    def mm(self, out, lhsT, rhs, start=True, stop=True):
        return self._record("pe", lambda e: e.matmul(out.ap, lhsT.ap, rhs.ap, start=start, stop=stop), [lhsT, rhs], [out])

    def transpose(self, out, in_, ident):
        return self._record("pe", lambda e: e.transpose(out.ap, in_.ap, ident.ap), [in_, ident], [out])

    def act(self, out, in_, func, bias=None, scale=None, accum_out=None):
        kw = {}
        reads = [in_]
        writes = [out]
        if bias is not None:
            if isinstance(bias, V):
                kw["bias"] = bias.ap
                reads.append(bias)
            else:
                kw["bias"] = bias
        if scale is not None:
            if isinstance(scale, V):
                kw["scale"] = scale.ap
                reads.append(scale)
            else:
                kw["scale"] = scale
        if accum_out is not None:
            kw["accum_out"] = accum_out.ap
            writes.append(accum_out)
        return self._record("act", lambda e: e.activation(out.ap, in_.ap, func, **kw), reads, writes)

    def tt(self, out, in0, in1, op, eng="dve"):
        return self._record(eng, lambda e: e.tensor_tensor(out.ap, in0.ap, in1.ap, op), [in0, in1], [out])

    def ts(self, out, in0, s1, s2, op0, op1=None, eng="dve"):
        reads = [in0]
        a1 = s1.ap if isinstance(s1, V) else s1
        a2 = s2.ap if isinstance(s2, V) else s2
        if isinstance(s1, V):
            reads.append(s1)
        if isinstance(s2, V):
            reads.append(s2)
        if op1 is None:
            return self._record(eng, lambda e: e.tensor_single_scalar(out.ap, in0.ap, a1, op0), reads, [out])
        return self._record(eng, lambda e: e.tensor_scalar(out.ap, in0.ap, a1, a2, op0, op1), reads, [out])

    def stt(self, out, in0, scalar, in1, op0, op1, eng="dve"):
        reads = [in0, in1]
        a = scalar.ap if isinstance(scalar, V) else scalar
        if isinstance(scalar, V):
            reads.append(scalar)
        return self._record(eng, lambda e: e.scalar_tensor_tensor(out.ap, in0.ap, a, in1.ap, op0, op1), reads, [out])

    def copy(self, out, in_, eng="dve"):
        if eng == "act":
            return self._record(eng, lambda e: e.copy(out.ap, in_.ap), [in_], [out])
        return self._record(eng, lambda e: e.tensor_copy(out.ap, in_.ap), [in_], [out])

    def memset(self, out, val, eng="dve"):
        return self._record(eng, lambda e: e.memset(out.ap, val), [], [out])

    def reduce(self, out, in_, op, axis=AX.X, eng="dve"):
        return self._record(eng, lambda e: e.tensor_reduce(out.ap, in_.ap, axis, op), [in_], [out])

    def recip(self, out, in_):
        return self._record("dve", lambda e: e.reciprocal(out.ap, in_.ap), [in_], [out])

    def generic(self, eng, fn, reads, writes):
        return self._record(eng, fn, reads, writes)

    def dma(self, out, in_, dsem, q="sp", **kw):
        return self._record(q, lambda e: e.dma_start(out=out.ap, in_=in_.ap, **kw), [in_], [out], is_dma=True, dsem=dsem)

    def finalize(self):
        for eng in ENGS:
            known = {}
            for op in self.streams[eng]:
                for d in op.deps:
                    if d.is_dma:
                        key = ("d", id(d.dsem))
                        if known.get(key, 0) >= d.semval:
                            continue
                        known[key] = d.semval
                        op.waits.append(d)
                    else:
                        key = ("e", d.eng)
                        if known.get(key, -1) >= d.idx:
                            continue
                        known[key] = d.idx
                        d.sig = True
                        op.waits.append(d)
        for eng in ENGS:
            c = 0
            for op in self.streams[eng]:
                if op.is_dma:
                    continue
                if op.sig:
                    c += 1
                    op.semval = c
        self.stats = {e: len(self.streams[e]) for e in ENGS}

    def emit(self):
        self.finalize()
        nc = self.nc
        prog = self
        with nc.Block() as block:
            def body(engname):
                def run(e):
                    for op in prog.streams[engname]:
                        for d in op.waits:
                            if d.is_dma:
                                e.wait_ge(d.dsem.sem, d.semval)
                            else:
                                e.wait_ge(prog.sems[d.eng], d.semval)
                        ins = op.fn(e)
                        if op.is_dma:
                            ins.then_inc(op.dsem.sem, op.inc)
                        elif op.sig:
                            ins.then_inc(prog.sems[op.eng], 1)
                    if engname == "sp":
                        for ds in prog.dsems:
                            if ds.count:
                                e.wait_ge(ds.sem, ds.count)
                return run
            block.tensor(body("pe"))
            block.scalar(body("act"))
            block.vector(body("dve"))
            block.gpsimd(body("pool"))
            block.sync(body("sp"))


def build_program(debug=(), nlayers=L, stop_after=None, skip_inputs=()):
    nc = bass.Bass("TRN2", target_bir_lowering=False)

    def din(name, shape, dt=F32):
        kind = "Internal" if name in skip_inputs else "ExternalInput"
        return nc.dram_tensor(name, list(shape), dt, kind=kind).ap()

    def scr(name, shape, dt):
        kind = "ExternalOutput" if name in debug else "Internal"
        return nc.dram_tensor(name, list(shape), dt, kind=kind).ap()

    xT = din("xT", [D, NT])
    w_in = din("w_in", [L, D, INW])
    w_a = din("w_br_a", [L, 512, D])
    w_b = din("w_br_b", [L, 1024, D])
    w_c = din("w_br_c", [L, 512, D])
    w_o = din("w_o", [L, D, D])
    w_1 = din("w_ff1", [L, D, DFF])
    w_2 = din("w_ff2", [L, DFF, D])
    rpe = din("rpe", [32, 4])
    cst_d = din("cst", [128, K_END])
    prm_d = din("prm", [128, L * NPRM])
    bc_d = din("bc", [L, 128, 3072])
    lrw_d = din("lrw", [L, 128, 512])
    sgw_d = din("sgw", [L, 128, 512])
    flags_d = din("flags", [128, 2])
    outT = nc.dram_tensor("outT", [D, NT], F32, kind="ExternalOutput").ap()

    hbuf = scr("hbuf", [D, NT], F32)
    qa_d = scr("qa_d", [512, NT], F32)
    ka_d = scr("ka_d", [512, NT], F32)
    va_d = scr("va_d", [NT, 512], BF16)
    qb_d = scr("qb_d", [512, NT], F32)
    kbF_d = scr("kbF_d", [512, NT], F32)
    kbT_d = scr("kbT_d", [NT, 512], F32)
    vb_d = scr("vb_d", [NT, 1024], BF16)
    rb_d = scr("rb_d", [1024, NT], BF16)
    lr_d = scr("lr_d", [16, NT], F32)
    uc_d = scr("uc_d", [512, NT], BF16)
    vc_d = scr("vc_d", [NT, 512], F32)
    gt_d = scr("gt_d", [6144, NT], BF16)
    ya_d = scr("ya_d", [512, NT], BF16)
    yb_d = scr("yb_d", [1024, NT], BF16)
    yc_d = scr("yc_d", [512, NT], BF16)
    kh_in = [scr(f"kh_in{l}", [512, NT], BF16) for l in range(L)]
    kh_out = [scr(f"kh_out{l}", [1024, NT], BF16) for l in range(L)]
    v_in = [scr(f"v_in{l}", [NT, 512], BF16) for l in range(L)]
    v_out = [scr(f"v_out{l}", [2 * NT, 512], BF16) for l in range(L)]
    st_in = [scr(f"st_in{l}", [512, 256], F32) for l in range(L)]
    st_out = [scr(f"st_out{l}", [1024, 256], F32) for l in range(L)]
    b_khin = [Buf() for _ in range(L)]
    b_khout = [Buf() for _ in range(L)]
    b_vin = [Buf() for _ in range(L)]
    b_vout = [Buf() for _ in range(L)]
    b_stin = [Buf() for _ in range(L)]
    b_stout = [Buf() for _ in range(L)]
    vv_t = nc.dram_tensor("vv_d", [4, 384], F32, kind="Internal")
    vv_d = vv_t.ap()

    with ExitStack() as st:
        P = Prog(nc, st)
        cf = P.sb([128, 512], F32)
        cb = P.sb([128, K_E32], BF16)
        prm = P.sb([128, L * NPRM], F32)
        Fb = P.sb([128, 8, 128], BF16)
        flags = P.sb([128, 2], F32)
        pm = P.sb([128, 256], F32)
        tri4 = P.sb([128, 512], F32)
        kmean_all = P.sb([128, 32], F32)
        pan = [P.sb([128, 8192], BF16) for _ in range(2)]
        pan_ds = [P.dsem() for _ in range(2)]
        stf = [P.sb([128, 512], F32) for _ in range(2)]
        stf_ds = [P.dsem() for _ in range(2)]
        stb = [P.sb([128, 512], BF16) for _ in range(2)]
        stb_ds = [P.dsem() for _ in range(2)]
        P.make_arena("X", 64 * 1024)
        P.make_arena("Y", 56 * 1024)
        banks = [P.ps([128, 512], F32) for _ in range(8)]
        NMISC = 48
        misc_ds = [P.dsem() for _ in range(NMISC)]
        st_ctr = {"f": 0, "b": 0, "p": 0, "bank": 0, "m": 0}

        ident_f = cf[:, 0:128]
        tri_f = cf[:, 128:256]
        uneg_f = cf[:, 256:384]
        lstr_f = cf[:, 384:512]
        ident_b = cb[:, K_ID:K_ID + 128]
        J_b = cb[:, K_J:K_J + 128]
        ones_b = cb[:, K_ONE:K_ONE + 128]

        reserved = []

        def nbank():
            while True:
                st_ctr["bank"] = (st_ctr["bank"] + 1) % 8
                b_ = banks[st_ctr["bank"]]
                if not any(b_ is r_ for r_ in reserved):
                    return b_

        def mds():
            st_ctr["m"] = (st_ctr["m"] + 1) % NMISC
            return misc_ds[st_ctr["m"]]

        def stage(kind):
            if kind == "f":
                i = st_ctr["f"] = (st_ctr["f"] + 1) % 2
                return stf[i], stf_ds[i]
            i = st_ctr["b"] = (st_ctr["b"] + 1) % 2
            return stb[i], stb_ds[i]

        def load_panel(parts):
            i = st_ctr["p"] = (st_ctr["p"] + 1) % 2
            slot, ds = pan[i], pan_ds[i]
            ktot = sum(p[1] for p in parts)
            cw = parts[0][2].shape[2]
            view = slot.t[:, 0:ktot * cw].rearrange("p (k c) -> p k c", k=ktot)
            for (k0, kc, src) in parts:
                P.dma(slot.v(view[:, k0:k0 + kc, :]), DV(src), ds, q="pool")
            return slot, view

        def wview(w2d):
            return w2d.rearrange("(k p) c -> p k c", p=128)

        hres0 = [P.alloc("X", T, F32) for _ in range(16)]
        for k_ in range(16):
            P.dma(hres0[k_][:], DV(xT[k_ * 128:(k_ + 1) * 128, 0:T]), mds())
        c_all = P.alloc("Y", K_END, F32)
        P.dma(c_all[:], DV(cst_d), mds())
        P.dma(prm[:], DV(prm_d), mds())
        P.dma(flags[:], DV(flags_d), mds())
        P.copy(cb[:], c_all[:, 0:K_E32])
        P.copy(cf[:, 0:128], c_all[:, K_ID:K_ID + 128])
        P.copy(cf[:, 128:256], c_all[:, K_TRI:K_TRI + 128])
        P.copy(cf[:, 256:512], c_all[:, K_UN:K_UN + 256])
        P.copy(pm[:], c_all[:, K_PM:K_PM + 256])
        for h_ in range(4):
            P.copy(tri4[:, h_ * 128:(h_ + 1) * 128], c_all[:, K_TRI:K_TRI + 128])
        tab = P.alloc("Y", 4, F32)
        P.dma(tab.v(tab.t[0:32, :]), DV(rpe), mds())
        bk = nbank()
        P.mm(bk.v(bk.t[0:4, 0:384]), tab.v(tab.t[0:32, 0:4]), c_all.v(c_all.t[0:32, K_E32:K_E32 + 384]), start=True, stop=False)
        P.mm(bk.v(bk.t[0:4, 0:384]), c_all.v(c_all.t[0:1, K_ONE:K_ONE + 4]), c_all.v(c_all.t[0:1, K_NEG:K_NEG + 384]), start=False, stop=True)
        vvs = P.alloc("Y", 384, F32)
        P.copy(vvs.v(vvs.t[0:4, :]), bk.v(bk.t[0:4, 0:384]))
        vds = mds()
        P.dma(DV(vv_d), vvs.v(vvs.t[0:4, :]), vds)
        P.barrier()
        Ff = P.alloc("Y", 8 * 128, F32)
        fds = mds()
        for h in range(4):
            for dl in range(2):
                hap = bass.AP(vv_t, 384 * h + 1 + 128 * dl, [[1, 128], [1, 128]])
                P.dma(Ff[:, (h * 2 + dl) * 128:(h * 2 + dl + 1) * 128], DV(hap), fds)
        P.copy(Fb.v(Fb.t[:].rearrange("p a b -> p (a b)")), Ff[:])
        P.release("Y")

        def phase1(l, g, hres=None):
            tok0 = g * T
            hsrc = xT if l == 0 else hbuf
            po = l * NPRM
            if hres is None:
                hres = [P.alloc("X", T, F32) for _ in range(16)]
                for k in range(16):
                    P.dma(hres[k][:], DV(hsrc[k * 128:(k + 1) * 128, tok0:tok0 + T]), mds())
            xn = [P.alloc("Y", T, BF16) for _ in range(16)]
            R = P.alloc("Y", T, F32)
            sq = [P.alloc("Y", T, BF16) for _ in range(2)]
            ssb = [nbank(), nbank()]
            for k in range(16):
                P.act(sq[k % 2][:], hres[k][:], AF.Square)
                for sub in range(T // 512):
                    P.mm(ssb[sub][:], ones_b, sq[k % 2][:, sub * 512:(sub + 1) * 512], start=(k == 0), stop=(k == 15))
            for sub in range(T // 512):
                P.act(R[:, sub * 512:(sub + 1) * 512], ssb[sub][:], AF.Sqrt, bias=EPS, scale=1.0 / D)
            P.recip(R[:], R[:])
            for k in range(16):
                P.stt(xn[k][:], hres[k][:], prm[:, po + k:po + k + 1], R[:], ALU.mult, ALU.mult)
            P.release("X")

            wv = wview(w_in[l])
            ev_ctr = [0]

            def evac(kind, bank_v, rows, cols):
                if kind in ("copy32", "gelu32"):
                    stg, ds = stage("f")
                else:
                    stg, ds = stage("b")
                o = stg.v(stg.t[0:rows, 0:cols])
                if kind in ("copy32", "copy16"):
                    ev_ctr[0] += 1
                    P.copy(o, bank_v, eng="act" if ev_ctr[0] % 2 else "dve")
                elif kind == "silu":
                    P.act(o, bank_v, AF.Silu)
                elif kind == "sigmoid":
                    P.act(o, bank_v, AF.Sigmoid)
                else:
                    P.act(o, bank_v, AF.Gelu_apprx_tanh)
                return o, ds

            def fm_seg(c0, width, kind, dst, r0=0):
                for pc in range(0, width, 512):
                    cw = min(512, width - pc)
                    slot, view = load_panel([(0, 16, wv[:, :, c0 + pc:c0 + pc + cw])])
                    for ct in range(0, cw, 128):
                        m = min(128, cw - ct)
                        for sub in range(T // 512):
                            bk = nbank()
                            for k in range(16):
                                P.mm(bk.v(bk.t[0:m, :]), slot.v(view[:, k, ct:ct + m]), xn[k][:, sub * 512:(sub + 1) * 512],
                                     start=(k == 0), stop=(k == 15))
                            o, ds = evac(kind, bk.v(bk.t[0:m, :]), m, 512)
                            P.dma(DV(dst[r0 + pc + ct:r0 + pc + ct + m, tok0 + sub * 512:tok0 + (sub + 1) * 512]), o, ds)

            def tm_seg(c0, width, kind, dst):
                for pc in range(0, width, 512):
                    slot, view = load_panel([(0, 16, wv[:, :, c0 + pc:c0 + pc + 512])])
                    for tt_ in range(T // 128):
                        bk = nbank()
                        for k in range(16):
                            P.mm(bk[:], xn[k][:, tt_ * 128:(tt_ + 1) * 128], slot.v(view[:, k, :]), start=(k == 0), stop=(k == 15))
                        o, ds = evac(kind, bk[:], 128, 512)
                        P.dma(DV(dst[tok0 + tt_ * 128:tok0 + (tt_ + 1) * 128, pc:pc + 512]), o, ds)

            fm_seg(C_QA, 512, "copy32", qa_d)
            fm_seg(C_KA, 512, "copy32", ka_d)
            tm_seg(C_VA, 512, "copy16", v_in[l])
            fm_seg(C_QB, 512, "copy32", qb_d)
            fm_seg(C_KB, 512, "copy32", kbF_d)
            tm_seg(C_KB, 512, "copy32", kbT_d)
            tm_seg(C_VB, 1024, "copy16", vb_d)
            fm_seg(C_RB, 1024, "silu", rb_d)
            slot, view = load_panel([(0, 16, wv[:, :, C_LR + 16 - 512:C_LR + 16])])
            for sub in range(T // 512):
                bk = nbank()
                for k in range(16):
                    P.mm(bk.v(bk.t[0:16, :]), slot.v(view[:, k, 496:512]), xn[k][:, sub * 512:(sub + 1) * 512],
                         start=(k == 0), stop=(k == 15))
                o, ds = evac("copy32", bk.v(bk.t[0:16, :]), 16, 512)
                P.dma(DV(lr_d[0:16, tok0 + sub * 512:tok0 + (sub + 1) * 512]), o, ds)
            fm_seg(C_UC, 512, "gelu16", uc_d)
            tm_seg(C_VC, 512, "gelu32", vc_d)
            fm_seg(C_G, 6144, "sigmoid", gt_d)
            P.release("X", "Y")

        def cc_gather(src, dst, bsrc, bdst):
            ds = P.dsem()
            ds.no_fence = True
            P._record("pool", lambda e: e.collective_compute("AllGather", ALU.bypass, replica_groups=GROUPS,
                                                             ins=[src.opt()], outs=[dst.opt()]),
                      [V(src, [bsrc])], [V(dst, [bdst])], is_dma=True, dsem=ds, inc=1)

        def knorm(l):
            po = l * NPRM
            H4 = range(4)
            kf = [P.alloc("X", NT, F32) for _ in H4]
            sqb = [P.alloc("X", NT, BF16) for _ in H4]
            rr = [P.alloc("X", NT, F32) for _ in H4]
            kds = [mds() for _ in H4]
            for h in H4:
                P.dma(kf[h][:], DV(ka_d[h * 128:(h + 1) * 128, :]), kds[h])
            for h in H4:
                P.act(sqb[h][:], kf[h][:], AF.Square)
            for h in H4:
                for sub in range(NT // 512):
                    bk = nbank()
                    P.mm(bk[:], ones_b, sqb[h][:, sub * 512:(sub + 1) * 512])
                    P.act(rr[h][:, sub * 512:(sub + 1) * 512], bk[:], AF.Sqrt, bias=EPS, scale=1.0 / 128.0)
            for h in H4:
                P.recip(rr[h][:], rr[h][:])
            for h in H4:
                P.stt(kf[h][:], kf[h][:], prm[:, po + 33:po + 34], rr[h][:], ALU.mult, ALU.mult)
            for h in H4:
                for sub in range(NT // 512):
                    stg, ds = stage("b")
                    P.copy(stg[:], kf[h][:, sub * 512:(sub + 1) * 512], eng="act")
                    P.dma(DV(kh_in[l][h * 128:(h + 1) * 128, sub * 512:(sub + 1) * 512]), stg[:], ds)
            for h in H4:
                P.reduce(kmean_all[:, h * 8 + 4:h * 8 + 8], kf[h].v(kf[h].t[:].rearrange("p (n c) -> p n c", n=4)), ALU.add)
            P.release("X", "Y")

        def moba(l):
            po = l * NPRM
            V_all = P.alloc("X", 16 * 512, BF16)
            V3 = V_all.t[:].rearrange("p (n c) -> p n c", n=16)
            P.dma(V_all.v(V3[:, 0:8, :]), V(v_out[l][0:NT, :].rearrange("(n p) c -> p n c", p=128), [b_vout[l]]), mds())
            P.dma(V_all.v(V3[:, 8:16, :]), DV(v_in[l].rearrange("(n p) c -> p n c", p=128)), mds())
            qf4 = P.alloc("Y", 4 * NT, F32)
            qf3 = qf4.t[:].rearrange("p (h s) -> p h s", h=4)
            qb4 = P.alloc("X", 4 * NT, BF16)
            qb3 = qb4.t[:].rearrange("p (h s) -> p h s", h=4)
            kb4 = P.alloc("X", 4 * S, BF16)
            kb3 = kb4.t[:].rearrange("p (h s) -> p h s", h=4)
            sqb = [P.alloc("X", NT, BF16) for _ in range(4)]
            rr = [P.alloc("Y", NT, F32) for _ in range(4)]
            selT = P.alloc("X", 32 * 128, BF16)
            P.memset(selT[:], 0.0)
            PT = [P.alloc("Y", 512, BF16) for _ in range(3)]
            rec = P.alloc("Y", 512, F32)
            G_all = P.alloc("Y", 256, F32)
            vld = P.alloc("Y", 256, F32)
            sel_all = P.alloc("Y", 256, F32)
            sel_i = [Tl(sel_all.t[:, i * 8:(i + 1) * 8]) for i in range(32)]
            M8_all = P.alloc("Y", 256, F32)
            M8_i = [Tl(M8_all.t[:, i * 8:(i + 1) * 8]) for i in range(32)]
            P.dma(qf4.v(qf3), DV(qa_d.rearrange("(h p) s -> p h s", p=128)), mds())
            P.dma(kb4.v(kb3[:, :, 0:NT]), V(kh_out[l][0:512, :].rearrange("(h p) s -> p h s", p=128), [b_khout[l]]), mds())
            P.dma(kb4.v(kb3[:, :, NT:S]), DV(kh_in[l].rearrange("(h p) s -> p h s", p=128)), mds())
            H4 = range(4)
            qhs = [qf4.v(qf3[:, h, :]) for h in H4]
            for h in H4:
                P.act(sqb[h][:], qhs[h], AF.Square)
            for h in H4:
                for sub in range(NT // 512):
                    bk = nbank()
                    P.mm(bk[:], ones_b, sqb[h][:, sub * 512:(sub + 1) * 512])
                    P.act(rr[h][:, sub * 512:(sub + 1) * 512], bk[:], AF.Sqrt, bias=128.0 * EPS, scale=1.0)
            for h in H4:
                P.recip(rr[h][:], rr[h][:])
            for h in H4:
                P.stt(qhs[h], qhs[h], prm[:, po + 32:po + 33], rr[h][:], ALU.mult, ALU.mult)
            for h in H4:
                P.copy(qb4.v(qb3[:, h, :]), qhs[h], eng="act")
            for h in H4:
                P.reduce(kmean_all[:, h * 8:h * 8 + 4], kb4.v(kb3[:, h, 0:NT].rearrange("p (n c) -> p n c", n=4)), ALU.add)
            bG = nbank()
            for h in range(4):
                for lq in range(8):
                    i = h * 8 + lq
                    P.mm(bG[:, i * 8:(i + 1) * 8], qf4.v(qf3[:, h, lq * 128:(lq + 1) * 128]), kmean_all[:, h * 8:h * 8 + 8])
            P.tt(G_all[:], bG[:, 0:256], pm[:], ALU.add)
            Gv = G_all.t[:].rearrange("p (i n) -> p i n", n=8)
            P.ts(G_all.v(Gv[:, :, 0:4]), G_all.v(Gv[:, :, 0:4]), flags[:, 1:2], None, ALU.add)
            for i in range(32):
                P.generic("dve", lambda e, o=M8_i[i], i_=i: e.max(o.t, G_all.t[:, i_ * 8:(i_ + 1) * 8]), [G_all[:]], [M8_i[i][:]])
            for i in range(32):
                P.ts(sel_i[i][:], G_all[:, i * 8:(i + 1) * 8], M8_i[i][:, 2:3], None, ALU.is_ge)
            P.ts(vld[:], G_all[:], -1e29, None, ALU.is_gt)
            sel_full = V(sel_all.t[:], [t_.buf for t_ in sel_i])
            P.tt(sel_full, sel_full, vld[:], ALU.mult)
            P.ts(sel_full, sel_full, -1.0, 1e30, ALU.add, ALU.mult)
            for i4 in range(8):
                bT = nbank()
                for j in range(4):
                    i = i4 * 4 + j
                    P.transpose(bT.v(bT.t[0:8, j * 128:(j + 1) * 128]), V(sel_all.t[:, i * 8:(i + 1) * 8], [t_.buf for t_ in sel_i]), ident_f)
                P.copy(selT.v(selT.t[0:8, i4 * 512:(i4 + 1) * 512]), bT.v(bT.t[0:8, 0:512]), eng=("act" if i4 % 2 else "dve"))
            steps = []
            for h in range(4):
                for g in (2, 3):
                    nk = 4 * g + 4
                    for kt in range(nk):
                        steps.append((h, g, kt, nk))
            acc = {}

            def issue_scores(h, g, kt, nk):
                if kt == 0:
                    acc[(h, g)] = (nbank(), nbank())
                OT, DEN = acc[(h, g)]
                live = [b_ for pair in acc.values() for b_ in pair]
                qend = (4 * g + 4 - 8) * 128
                qlo = max(kt, 4 * g)
                c0 = (qlo - 4 * g) * 128
                STb = nbank()
                while any(STb is b_ for b_ in live):
                    STb = nbank()
                mms = [(STb[:, c0:512], kb4.v(kb3[:, h, kt * 128:(kt + 1) * 128]), qb4.v(qb3[:, h, (qlo - 8) * 128:qend]))]
                if kt >= 4 * g:
                    cc = (kt - 4 * g) * 128
                    mms.append((STb[:, cc:cc + 128], J_b, Fb.v(Fb.t[:, h * 2 + 0, :])))
                if 4 * g <= kt + 1 <= 4 * g + 3:
                    cc = (kt + 1 - 4 * g) * 128
                    mms.append((STb[:, cc:cc + 128], J_b, Fb.v(Fb.t[:, h * 2 + 1, :])))
                n = kt // 2
                qs = max(qlo, 2 * n + 2)
                if qs <= 4 * g + 3:
                    cc = (qs - 4 * g) * 128
                    mms.append((STb[:, cc:512], cb[:, K_EN + n * 128:K_EN + (n + 1) * 128],
                                selT[:, (h * 8 + qs - 8) * 128:h * 1024 + qend]))
                for i, (o_, l_, r_) in enumerate(mms):
                    P.mm(o_, l_, r_, start=(i == 0), stop=(i == len(mms) - 1))
                return STb, c0

            def issue_pv(h, g, kt, nk, STb, c0, idx):
                OT, DEN = acc[(h, g)]
                pt = PT[idx % 3]
                P.act(pt[:, c0:512], STb[:, c0:512], AF.Exp)
                P.mm(OT[:, c0:512], V_all.v(V3[:, kt, h * 128:(h + 1) * 128]), pt[:, c0:512], start=(kt == 0), stop=(kt == nk - 1))
                P.mm(DEN[:, c0:512], ones_b, pt[:, c0:512], start=(kt == 0), stop=(kt == nk - 1))
                if kt == nk - 1:
                    P.recip(rec[:], DEN[:])
                    stg, ds = stage("b")
                    P.tt(stg[:], OT[:], rec[:], ALU.mult)
                    P.dma(DV(ya_d[h * 128:(h + 1) * 128, (g - 2) * 512:(g - 1) * 512]), stg[:], ds)
                    del acc[(h, g)]

            pend = None
            for idx, stp in enumerate(steps):
                cur = issue_scores(*stp)
                if pend is not None:
                    issue_pv(*pend)
                pend = (*stp, cur[0], cur[1], idx)
            issue_pv(*pend)
            P.release("X", "Y")

        NCH = NT // 128

        def gla_gen(l, state_only, rel=True):
            po = l * NPRM
            la = P.alloc("Y", NCH * 512, F32)
            la3 = la.t[:].rearrange("p (n c) -> p n c", n=NCH)
            lrT = P.alloc("X", NT, F32)
            lrw = P.alloc("X", 512, F32)
            etmp = [P.alloc("X", 512, F32) for _ in range(2)]
            P.memset(lrT[:], 1.0)
            P.dma(lrT.v(lrT.t[0:16, :]), DV(lr_d), mds())
            P.dma(lrw[:], DV(lrw_d[l]), mds())
            la_c = [Tl(la3[:, c, :]) for c in range(NCH)]
            for tt_ in range(NCH):
                bk = nbank()
                P.mm(bk[:], lrT[:, tt_ * 128:(tt_ + 1) * 128], lrw[:])
                P.act(etmp[tt_ % 2][:], bk[:], AF.Exp, scale=-1.0)
                P.act(la_c[tt_][:], etmp[tt_ % 2][:], AF.Ln, bias=1.0, scale=1.0)
            ktc = [P.alloc("X", 512, F32) for _ in range(2)]
            vtc = [P.alloc("X", 1024, BF16) for _ in range(2)]
            E3 = [P.alloc("Y", 512, F32) for _ in range(2)]
            ke = [P.alloc("Y", 512, BF16) for _ in range(2)]
            Sf = P.alloc("Y", 1024, F32)
            Sf_h = [Tl(Sf.t[:, h * 256:(h + 1) * 256]) for h in range(4)]
            Sf_full = V(Sf.t[:], [t_.buf for t_ in Sf_h])
            lds = [[mds() for _ in range(2)] for _ in range(5)]
            if state_only:
                dec = [P.alloc("Y", 4, F32) for _ in range(2)]
                P.memset(Sf_full, 0.0)
            else:
                qfc = [P.alloc("X", 512, F32) for _ in range(2)]
                kfc = [P.alloc("X", 512, F32) for _ in range(2)]
                rTc = [P.alloc("X", 1024, BF16) for _ in range(2)]
                t1 = [P.alloc("X", 1024, F32) for _ in range(2)]
                ybc = [P.alloc("X", 1024, BF16) for _ in range(2)]
                yb_ds = [mds(), mds()]
                E1 = [P.alloc("Y", 512, F32) for _ in range(2)]
                E2 = [P.alloc("Y", 512, F32) for _ in range(2)]
                qd = [P.alloc("Y", 512, BF16) for _ in range(2)]
                ki = [P.alloc("Y", 512, BF16) for _ in range(2)]
                am = [P.alloc("Y", 512, BF16) for _ in range(2)]
                Sb = P.alloc("Y", 1024, BF16)
                sqo = [P.alloc("Y", 1024, BF16) for _ in range(2)]
                rr = [P.alloc("Y", 512, F32) for _ in range(2)]
                sds = mds()
                P.dma(V(Sf.t[:].rearrange("p (h c) -> p h c", h=4), Sf_full.bufs), V(st_out[l][0:512, :].rearrange("(h p) c -> p h c", p=128), [b_stout[l]]), sds)
                P.ts(Sf_full, Sf_full, flags[:, 0:1], None, ALU.mult)
                P.copy(Sb[:], Sf_full, eng="act")
            if not state_only:
                o_sb = [P.alloc("X", 1024, F32) for _ in range(2)]

            def stage_f(c):
                pr = c % 2
                cs = slice(c * 128, (c + 1) * 128)
                P.dma(ktc[pr][:], DV(kbT_d[cs, :]), lds[0][pr])
                P.dma(vtc[pr][:], DV(vb_d[cs, :]), lds[1][pr])
                if not state_only:
                    P.dma(qfc[pr].v(qfc[pr].t[:].rearrange("p (h s) -> p h s", h=4)), DV(qb_d.rearrange("(h p) s -> p h s", p=128)[:, :, cs]), lds[2][pr])
                    P.dma(kfc[pr].v(kfc[pr].t[:].rearrange("p (h s) -> p h s", h=4)), DV(kbF_d.rearrange("(h p) s -> p h s", p=128)[:, :, cs]), lds[3][pr])
                lac = la_c[c]
                bB = banks[0]
                for h in range(4):
                    P.mm(bB[:, h * 128:(h + 1) * 128], lstr_f, lac[:, h * 128:(h + 1) * 128])
                bA = banks[1]
                if state_only:
                    for h in range(4):
                        P.mm(bA[:, h:h + 1], lac[:, h * 128:(h + 1) * 128], cf[:, 383:384])
                else:
                    for h in range(4):
                        P.mm(bA[:, h * 128:(h + 1) * 128], lac[:, h * 128:(h + 1) * 128], uneg_f)
                P.act(E3[pr][:], bB[:], AF.Exp)
                P.tt(ke[pr][:], ktc[pr][:], E3[pr][:], ALU.mult)
                if state_only:
                    P.act(dec[pr][:], bA[:, 0:4], AF.Exp)
                    decay = [dec[pr][:, h:h + 1] for h in range(4)]
                else:
                    P.act(E1[pr][:], bA[:], AF.Exp)
                    P.act(E2[pr][:], bA[:], AF.Exp, scale=-1.0)
                    decay = [E1[pr][:, h * 128 + 127:h * 128 + 128] for h in range(4)]
                    P.stt(qd[pr][:], qfc[pr][:], 128.0 ** -0.5, E1[pr][:], ALU.mult, ALU.mult)
                    P.tt(ki[pr][:], kfc[pr][:], E2[pr][:], ALU.mult)

            def stage_m(c):
                pr = c % 2
                if state_only:
                    decay = [dec[pr][:, h:h + 1] for h in range(4)]
                else:
                    decay = [E1[pr][:, h * 128 + 127:h * 128 + 128] for h in range(4)]
                if not state_only:
                    cs = slice(c * 128, (c + 1) * 128)
                    P.dma(rTc[pr].v(rTc[pr].t[:].rearrange("p (j s) -> p j s", j=8)), DV(rb_d.rearrange("(j p) s -> p j s", p=128)[:, :, cs]), lds[4][pr])
                    bC = banks[2]
                    for h in range(4):
                        hs = slice(h * 128, (h + 1) * 128)
                        P.mm(bC[:, hs], ki[pr][:, hs], qd[pr][:, hs])
                    P.tt(am[pr][:], bC[:], tri4[:], ALU.mult)
                    bO = [banks[3], banks[4]]
                    for h in range(4):
                        hs = slice(h * 128, (h + 1) * 128)
                        for dv in range(2):
                            o_ = bO[h // 2][:, (h % 2) * 256 + dv * 128:(h % 2) * 256 + (dv + 1) * 128]
                            P.mm(o_, vtc[pr][:, h * 256 + dv * 128:h * 256 + (dv + 1) * 128], am[pr][:, hs], start=True, stop=False)
                            P.mm(o_, Sb[:, h * 256 + dv * 128:h * 256 + (dv + 1) * 128], qd[pr][:, hs], start=False, stop=True)
                bS = [banks[5], banks[6]]
                for h in range(4):
                    P.mm(bS[h // 2][:, (h % 2) * 256:(h % 2 + 1) * 256], ke[pr][:, h * 128:(h + 1) * 128], vtc[pr][:, h * 256:(h + 1) * 256])
                for h in range(4):
                    P.stt(Sf_h[h][:], Sf_h[h][:], decay[h], bS[h // 2][:, (h % 2) * 256:(h % 2 + 1) * 256], ALU.mult, ALU.add)
                if state_only:
                    return
                P.copy(Sb[:], Sf_full, eng="act")
                for half in range(2):
                    P.act(sqo[pr][:, half * 512:(half + 1) * 512], bO[half][:], AF.Square)
                    P.copy(o_sb[pr][:, half * 512:(half + 1) * 512], bO[half][:], eng="act")

            def stage_b(c):
                pr = c % 2
                cs = slice(c * 128, (c + 1) * 128)
                bR = banks[7]
                for h in range(4):
                    P.mm(bR[:, h * 128:(h + 1) * 128], ones_b, sqo[pr][:, h * 256:h * 256 + 128], start=True, stop=False)
                    P.mm(bR[:, h * 128:(h + 1) * 128], ones_b, sqo[pr][:, h * 256 + 128:h * 256 + 256], start=False, stop=True)
                P.act(rr[pr][:], bR[:], AF.Sqrt, bias=EPS, scale=1.0 / 256.0)
                P.recip(rr[pr][:], rr[pr][:])
                t1v = t1[pr].t[:].rearrange("p (a b c) -> p a b c", a=4, b=2)
                ov = o_sb[pr].t[:].rearrange("p (a b c) -> p a b c", a=4, b=2)
                rrv = rr[pr].t[:].rearrange("p (a c) -> p a c", a=4)
                for dv in range(2):
                    P.stt(t1[pr].v(t1v[:, :, dv, :]), o_sb[pr].v(ov[:, :, dv, :]),
                          prm[:, po + 34 + dv:po + 35 + dv], rr[pr].v(rrv), ALU.mult, ALU.mult)
                P.tt(ybc[pr][:], t1[pr][:], rTc[pr][:], ALU.mult)
                P.dma(DV(yb_d.rearrange("(j p) s -> p j s", p=128)[:, :, cs]), ybc[pr].v(ybc[pr].t[:].rearrange("p (j s) -> p j s", j=8)), yb_ds[pr])

            yield
            if state_only:
                stage_f(0)
                for c in range(NCH):
                    if c + 1 < NCH:
                        stage_f(c + 1)
                    stage_m(c)
                    yield
            else:
                stage_f(0)
                for c in range(NCH):
                    if c + 1 < NCH:
                        stage_f(c + 1)
                    stage_m(c)
                    if c >= 1:
                        stage_b(c - 1)
                stage_b(NCH - 1)
            if state_only:
                P.dma(DV(st_in[l].rearrange("(h p) c -> p h c", p=128)), V(Sf.t[:].rearrange("p (h c) -> p h c", h=4), Sf_full.bufs), mds())
            if rel:
                P.release("X", "Y")

        def gla(l, state_only):
            for _ in gla_gen(l, state_only):
                pass

        def gmlp_gen(l, rel=True):
            ucT = P.alloc("X", 4 * NT, BF16)
            uc3 = ucT.t[:].rearrange("p (g s) -> p g s", g=4)
            P.dma(ucT.v(uc3), DV(uc_d.rearrange("(g p) s -> p g s", p=128)), mds())
            bcl = P.alloc("X", 3072, F32)
            P.dma(bcl[:], DV(bc_d[l]), mds())
            ws = P.alloc("X", 512, F32)
            P.dma(ws[:], DV(sgw_d[l]), mds())
            wsb = P.alloc("X", 512, BF16)
            P.tt(wsb[:], ws[:], tri4[:], ALU.mult)
            vt = [P.alloc("Y", 512, F32) for _ in range(4)]
            vds = [mds() for _ in range(4)]
            junk = [P.alloc("Y", 512, BF16) for _ in range(4)]
            vn = [P.alloc("Y", 512, F32) for _ in range(4)]
            vnb = [P.alloc("Y", 512, BF16) for _ in range(4)]
            tmp = [P.alloc("Y", 512, F32) for _ in range(2)]
            s1 = [P.alloc("Y", 1, F32) for _ in range(4)]
            s2 = [P.alloc("Y", 1, F32) for _ in range(4)]
            mu = [P.alloc("Y", 1, F32) for _ in range(4)]
            m2 = [P.alloc("Y", 1, F32) for _ in range(4)]
            var = [P.alloc("Y", 1, F32) for _ in range(4)]
            R4 = range(4)
            yield
            for c4 in range(NCH // 4):
                gb = [nbank() for _ in range(4)]
                for cc in R4:
                    c = c4 * 4 + cc
                    P.dma(vt[cc][:], DV(vc_d[c * 128:(c + 1) * 128, :]), vds[cc])
                for cc in R4:
                    P.reduce(s1[cc][:], vt[cc][:], ALU.add)
                yield
                for cc in R4:
                    P.act(junk[cc][:], vt[cc][:], AF.Square, accum_out=s2[cc][:])
                for cc in R4:
                    P.ts(mu[cc][:], s1[cc][:], 1.0 / 512.0, None, ALU.mult)
                yield
                for cc in R4:
                    P.tt(m2[cc][:], mu[cc][:], mu[cc][:], ALU.mult)
                for cc in R4:
                    P.stt(var[cc][:], s2[cc][:], 1.0 / 512.0, m2[cc][:], ALU.mult, ALU.subtract)
                yield
                for cc in R4:
                    P.act(var[cc][:], var[cc][:], AF.Sqrt, bias=EPS, scale=1.0)
                for cc in R4:
                    P.recip(var[cc][:], var[cc][:])
                yield
                for cc in R4:
                    P.ts(vn[cc][:], vt[cc][:], mu[cc][:, 0:1], var[cc][:, 0:1], ALU.subtract, ALU.mult)
                for cc in R4:
                    P.tt(vn[cc][:], vn[cc][:], bcl[:, 0:512], ALU.mult)
                yield
                for cc in R4:
                    P.tt(vnb[cc][:], vn[cc][:], bcl[:, 512:1024], ALU.add)
                for cc in R4:
                    for gi in range(4):
                        P.mm(gb[gi][:, cc * 128:(cc + 1) * 128], vnb[cc][:, gi * 128:(gi + 1) * 128], wsb[:, gi * 128:(gi + 1) * 128])
                yield
                for gi in range(4):
                    P.tt(tmp[gi % 2][:], gb[gi][:], bcl[:, 1024 + gi * 512:1024 + (gi + 1) * 512], ALU.add)
                    stg, ds = stage("b")
                    P.tt(stg[:], tmp[gi % 2][:], ucT.v(uc3[:, gi, c4 * 512:(c4 + 1) * 512]), ALU.mult)
                    P.dma(DV(yc_d[gi * 128:(gi + 1) * 128, c4 * 512:(c4 + 1) * 512]), stg[:], ds)
                yield
            if rel:
                P.release("X", "Y")

        def gmlp(l):
            for _ in gmlp_gen(l):
                pass

        def gla_gmlp(l):
            reserved.extend([banks[0], banks[1], banks[5], banks[6]])
            alive = [gla_gen(l, True, rel=False), gmlp_gen(l, rel=False)]
            while alive:
                for g_ in list(alive):
                    try:
                        next(g_)
                    except StopIteration:
                        alive.remove(g_)
            del reserved[:]
            P.release("X", "Y")

        def phase3(l, g, last, pre=None):
            tok0 = g * T
            po = l * NPRM
            hsrc = xT if l == 0 else hbuf
            hdst = outT if last else hbuf
            NS = T // 512
            y = [P.alloc("X", T, BF16) for _ in range(16)]
            for k in range(16):
                yd = mds()
                if k < 4:
                    src = ya_d[k * 128:(k + 1) * 128, tok0:tok0 + T]
                elif k < 12:
                    src = yb_d[(k - 4) * 128:(k - 3) * 128, tok0:tok0 + T]
                else:
                    src = yc_d[(k - 12) * 128:(k - 11) * 128, tok0:tok0 + T]
                P.dma(y[k][:], DV(src), yd)
            mg = [P.alloc("Y", T, BF16) for _ in range(16)]
            G6 = P.alloc("Y", 6 * 512, BF16)
            gsl = [Tl(G6.t[:, i * 512:(i + 1) * 512]) for i in range(6)]
            gds = [mds() for _ in range(6)]
            gi_ = 0

            def br_panel(cbk):
                return load_panel([(0, 4, wview(w_a[l])[:, :, cbk * 512:(cbk + 1) * 512]),
                                   (4, 8, wview(w_b[l])[:, :, cbk * 512:(cbk + 1) * 512]),
                                   (12, 4, wview(w_c[l])[:, :, cbk * 512:(cbk + 1) * 512])])
            nxt = pre if pre is not None else br_panel(0)
            TB = P.alloc("Y", 6 * 512, F32)
            tmpsets = [tuple(Tl(TB.t[:, (j * 3 + i) * 512:(j * 3 + i + 1) * 512]) for i in range(3)) for j in range(2)]
            it = 0
            for cbk in range(4):
                slot, view = nxt
                if cbk + 1 < 4:
                    nxt = br_panel(cbk + 1)
                for ct in range(4):
                    dt_ = cbk * 4 + ct
                    for sub in range(NS):
                        ss = slice(sub * 512, (sub + 1) * 512)
                        tmps = tmpsets[it % 2]
                        it += 1
                        for br, (k0, k1) in enumerate(((0, 4), (4, 12), (12, 16))):
                            gi_ = (gi_ + 1) % 6
                            gt = gsl[gi_]
                            P.dma(gt[:], DV(gt_d[br * 2048 + dt_ * 128: br * 2048 + (dt_ + 1) * 128, tok0 + sub * 512:tok0 + (sub + 1) * 512]), gds[gi_])
                            bk = nbank()
                            for k in range(k0, k1):
                                P.mm(bk[:], slot.v(view[:, k, ct * 128:(ct + 1) * 128]), y[k][:, ss], start=(k == k0), stop=(k == k1 - 1))
                            P.tt(tmps[br][:], bk[:], gt[:], ALU.mult)
                        P.tt(tmps[0][:], tmps[0][:], tmps[1][:], ALU.add)
                        P.tt(mg[dt_][:, ss], tmps[0][:], tmps[2][:], ALU.add, eng="pool")
            wo_first = load_panel([(0, 16, wview(w_o[l])[:, :, 0:512])])
            P.release("X")
            hT = [P.alloc("X", T, F32) for _ in range(16)]
            hstg = [Tl(TB.t[:, i * 512:(i + 1) * 512]) for i in range(3)]
            hds = [mds() for _ in range(3)]
            sq = [Tl(G6.t[:, i * 1024:(i + 1) * 1024]) for i in range(2)]
            ssb = [nbank() for _ in range(NS)]
            reserved.extend(ssb)
            hi_ = 0
            nxt = wo_first
            for cbk in range(4):
                slot, view = nxt
                if cbk + 1 < 4:
                    nxt = load_panel([(0, 16, wview(w_o[l])[:, :, (cbk + 1) * 512:(cbk + 2) * 512])])
                for ct in range(4):
                    dt_ = cbk * 4 + ct
                    for sub in range(NS):
                        ss = slice(sub * 512, (sub + 1) * 512)
                        hi_ = (hi_ + 1) % 3
                        P.dma(hstg[hi_][:], DV(hsrc[dt_ * 128:(dt_ + 1) * 128, tok0 + sub * 512:tok0 + (sub + 1) * 512]), hds[hi_])
                        bk = nbank()
                        for k in range(16):
                            P.mm(bk[:], slot.v(view[:, k, ct * 128:(ct + 1) * 128]), mg[k][:, ss], start=(k == 0), stop=(k == 15))
                        P.tt(hT[dt_][:, ss], bk[:], hstg[hi_][:], ALU.add)
                    if dt_ >= 1:
                        for sub in range(NS):
                            P.mm(ssb[sub][:], ones_b, sq[(dt_ - 1) % 2][:, sub * 512:(sub + 1) * 512], start=(dt_ == 1), stop=False)
                    P.act(sq[dt_ % 2][:], hT[dt_][:], AF.Square)
            for sub in range(NS):
                P.mm(ssb[sub][:], ones_b, sq[15 % 2][:, sub * 512:(sub + 1) * 512], start=False, stop=True)
            ffn_first = load_panel([(0, 16, wview(w_1[l])[:, :, 0:512])])
            P.release("Y")
            del reserved[:]
            hn = [P.alloc("Y", T, BF16) for _ in range(16)]
            aT = [P.alloc("Y", T, BF16) for _ in range(8)]
            R2 = P.alloc("Y", T, F32)
            for sub in range(NS):
                P.act(R2[:, sub * 512:(sub + 1) * 512], ssb[sub][:], AF.Sqrt, bias=EPS, scale=1.0 / D)
            P.recip(R2[:], R2[:])
            for k in range(16):
                P.stt(hn[k][:], hT[k][:], prm[:, po + 16 + k:po + 17 + k], R2[:], ALU.mult, ALU.mult)
            rtmp = [stf[0], stf[1]]
            for fb in range(8):
                for half in range(2):
                    c0 = fb * 1024 + half * 512
                    if fb == 0 and half == 0:
                        slot, view = ffn_first
                    else:
                        slot, view = load_panel([(0, 16, wview(w_1[l])[:, :, c0:c0 + 512])])
                    for ct in range(4):
                        fc = half * 4 + ct
                        for sub in range(NS):
                            ss = slice(sub * 512, (sub + 1) * 512)
                            bk = nbank()
                            for k in range(16):
                                P.mm(bk[:], slot.v(view[:, k, ct * 128:(ct + 1) * 128]), hn[k][:, ss], start=(k == 0), stop=(k == 15))
                            rt = rtmp[(fc * NS + sub) % 2]
                            P.act(rt[:], bk[:], AF.Relu)
                            P.tt(aT[fc][:, ss], rt[:], rt[:], ALU.mult)
                for j in range(2):
                    slot, view = load_panel([(0, 8, wview(w_2[l])[:, fb * 8:(fb + 1) * 8, j * 1024:(j + 1) * 1024])])
                    for ct in range(8):
                        dt_ = j * 8 + ct
                        for sub in range(NS):
                            ss = slice(sub * 512, (sub + 1) * 512)
                            bk = nbank()
                            for k in range(8):
                                P.mm(bk[:], slot.v(view[:, k, ct * 128:(ct + 1) * 128]), aT[k][:, ss], start=(k == 0), stop=(k == 7))
                            P.tt(hT[dt_][:, ss], bk[:], hT[dt_][:, ss], ALU.add)
            ods = [mds() for _ in range(4)]
            for dt_ in range(16):
                P.dma(DV(hdst[dt_ * 128:(dt_ + 1) * 128, tok0:tok0 + T]), hT[dt_][:], ods[dt_ % 4])
            if last:
                P.release("X", "Y")
                return None
            P.release("Y")
            return hT

        def first_branch_panel(l):
            return load_panel([(0, 4, wview(w_a[l])[:, :, 0:512]),
                               (4, 8, wview(w_b[l])[:, :, 0:512]),
                               (12, 4, wview(w_c[l])[:, :, 0:512])])

        hres = hres0
        for l in range(nlayers):
            for g in range(NG):
                phase1(l, g, hres)
            if stop_after == ("p1", l):
                break
            cc_gather(v_in[l], v_out[l], b_vin[l], b_vout[l])
            knorm(l)
            cc_gather(kh_in[l], kh_out[l], b_khin[l], b_khout[l])
            gla_gmlp(l)
            cc_gather(st_in[l], st_out[l], b_stin[l], b_stout[l])
            moba(l)
            pre = None
            gla(l, False)
            if stop_after == ("p2", l):
                break
            for g in range(NG):
                hres = phase3(l, g, last=(l == nlayers - 1), pre=pre)
        P.emit()
        build_program.stats = P.stats
    return nc


def _rpe_bucket(d):
    n = np.maximum(d, 0)
    max_exact = 16
    nf = np.maximum(n, 1).astype(np.float32)
    large = max_exact + (np.log(nf / np.float32(max_exact)) / np.float32(math.log(128 / max_exact)) * np.float32(32 - max_exact)).astype(np.int32)
    large = np.minimum(large, 31)
    return np.where(n < max_exact, n, large)


def _consts():
    c = np.zeros((128, K_END), np.float32)
    i = np.arange(128)
    c[:, K_ID:K_ID + 128] = np.eye(128, dtype=np.float32)
    c[:, K_J:K_J + 128] = np.eye(128, dtype=np.float32)[::-1]
    c[:, K_ONE:K_ONE + 128] = 1.0
    tri = (i[:, None] <= i[None, :]).astype(np.float32)
    c[:, K_TRI:K_TRI + 128] = tri
    c[:, K_UN:K_UN + 128] = tri * (-1.0 / 16.0)
    c[:, K_LS:K_LS + 128] = (i[:, None] > i[None, :]).astype(np.float32) * (-1.0 / 16.0)
    for n in range(8):
        c[n, K_EN + n * 128:K_EN + (n + 1) * 128] = 1.0
    m = np.arange(384)
    d = m - 128
    bkt = _rpe_bucket(d)
    for mm_ in range(128, 384):
        c[bkt[mm_], K_E32 + mm_] += 1.0
        c[31, K_E32 + mm_] -= 1.0
    c[0, K_NEG:K_NEG + 128] = -1e30
    for h in range(4):
        for lq in range(8):
            qblk = (8 + lq) // 2
            for n in range(qblk, 8):
                c[:, K_PM + (h * 8 + lq) * 8 + n] = -1e30
    return c


_NC_CACHE = {}


def _prep_inputs(inp):
    f = lambda a: np.ascontiguousarray(np.asarray(a, dtype=np.float32))
    prm = np.zeros((128, L * NPRM), np.float32)
    bc = np.zeros((L, 128, 3072), np.float32)
    lrw = np.zeros((L, 128, 512), np.float32)
    sgw = np.zeros((L, 128, 512), np.float32)
    for l in range(L):
        o = l * NPRM
        prm[:, o:o + 16] = f(inp["norm1_g"])[l].reshape(16, 128).T
        prm[:, o + 16:o + 32] = f(inp["norm2_g"])[l].reshape(16, 128).T
        prm[:, o + 32] = f(inp["q_norm_g"])[l]
        prm[:, o + 33] = f(inp["k_norm_g"])[l]
        prm[:, o + 34:o + 36] = f(inp["gla_out_g"])[l].reshape(2, 128).T
        bc[l, :, 0:512] = f(inp["sg_ln_g"])[l][None, :]
        bc[l, :, 512:1024] = f(inp["sg_ln_b"])[l][None, :]
        for gi in range(4):
            bc[l, :, 1024 + gi * 512:1024 + (gi + 1) * 512] = np.tile(f(inp["sg_b"])[l, gi], 4)[None, :]
        lrw[l, 0:16] = f(inp["gla_lr_w"])[l]
        lrw[l, 16] = f(inp["gla_lr_b"])[l]
        for gi in range(4):
            sgw[l, :, gi * 128:(gi + 1) * 128] = f(inp["sg_w"])[l, gi].T
    shared = {
        "w_in": f(inp["w_in"]), "w_br_a": f(inp["w_br_a"]), "w_br_b": f(inp["w_br_b"]), "w_br_c": f(inp["w_br_c"]),
        "w_o": f(inp["w_o"]), "w_ff1": f(inp["w_ff1"]), "w_ff2": f(inp["w_ff2"]),
        "rpe": f(inp["rpe_table"]), "cst": _consts(), "prm": prm, "bc": bc, "lrw": lrw, "sgw": sgw,
    }
    x = f(inp["x"])
    in_maps = []
    for c in range(8):
        b, half = c // 2, c % 2
        m = dict(shared)
        m["xT"] = np.ascontiguousarray(x[b, half * NT:(half + 1) * NT].T)
        fl = np.zeros((128, 2), np.float32)
        fl[:, 0] = 1.0 if half == 1 else 0.0
        fl[:, 1] = 0.0 if half == 1 else -1e30
        m["flags"] = fl
        in_maps.append(m)
    return in_maps


def kernel(**inputs):
    in_maps = _prep_inputs(inputs)
    if "nc" not in _NC_CACHE:
        _NC_CACHE["nc"] = build_program()
    nc = _NC_CACHE["nc"]
    res = run_bass_kernel_spmd(nc, in_maps, core_ids=list(range(8)))
    out = np.empty((4, S, D), np.float32)
    for c in range(8):
        b, half = c // 2, c % 2
        out[b, half * NT:(half + 1) * NT, :] = np.asarray(res.results[c]["outT"]).T
    return out
```
